# Optimizing a Trainium2 kernel written in Bass

```python
import jax, jax.numpy as jnp
from jax import lax
import numpy as np

D_MODEL = 1024
BATCH = 32
SEQ = 256
DEPTH = 2
DEC_BATCH = 2
DEC_SEQ = 4096
PAST_LEN = 256

GRID_W = 64
N_MIXERS = 2
HGRN_EXPAND = 128
HGRN_HEADS = D_MODEL // HGRN_EXPAND
HGRN_DK = HGRN_EXPAND
HGRN_DV = D_MODEL // HGRN_HEADS
CHUNK = 32
N_HGRN = (DEPTH + N_MIXERS - 1) // N_MIXERS
N_CONV = DEPTH // N_MIXERS
CONV_W = 3
D_FF = 2816
N_MOD = 6
EPS = 1e-6

kernel_name = "hybrid_hgrn2_shortconv_dit_step"


def rms_norm(x, w):
    xf = x.astype(jnp.float32)
    y = xf * lax.rsqrt(jnp.mean(xf * xf, axis=-1, keepdims=True) + EPS)
    return (y * w.astype(jnp.float32)).astype(x.dtype)


def modulate(x, w, shift, scale):
    return rms_norm(x, w) * (1.0 + scale) + shift


def dwconv3(u, w, b, axis):
    n = u.shape[axis]
    pad = [(0, 0)] * u.ndim
    pad[axis] = (1, 1)
    up = jnp.pad(u, pad)
    prev = lax.slice_in_dim(up, 0, n, axis=axis)
    nxt = lax.slice_in_dim(up, 2, n + 2, axis=axis)
    return prev * w[0] + u * w[1] + nxt * w[2] + b


def token_conv(u, w, b, grid_axis):
    if grid_axis is None:
        return dwconv3(u, w, b, axis=1)
    bsz, n, ch = u.shape
    rows = n // GRID_W
    ug = u.reshape(bsz, rows, GRID_W, ch)
    return dwconv3(ug, w, b, axis=grid_axis).reshape(bsz, n, ch)


def gla_chunk(q, k, v, log_f, s0):
    bsz, nh, n, dk = q.shape
    dv = v.shape[-1]
    nc = n // CHUNK
    q, k, log_f = (t.reshape(bsz, nh, nc, CHUNK, dk) for t in (q, k, log_f))
    v = v.reshape(bsz, nh, nc, CHUNK, dv)
    g_cum = jnp.cumsum(log_f, axis=3)
    g_last = g_cum[:, :, :, -1:, :]
    q_dec = q * jnp.exp(g_cum)
    k_intra = k * jnp.exp(-g_cum)
    k_state = k * jnp.exp(g_last - g_cum)
    lower = jnp.tril(jnp.ones((CHUNK, CHUNK), dtype=bool))
    scores = jnp.einsum("bhncd,bhnsd->bhncs", q_dec, k_intra)
    scores = jnp.where(lower, scores, 0.0)
    o_intra = jnp.einsum("bhncs,bhnse->bhnce", scores, v)
    chunk_kv = jnp.einsum("bhnsd,bhnse->nbhde", k_state, v)
    chunk_decay = jnp.moveaxis(jnp.exp(g_last[:, :, :, 0, :]), 2, 0)

    def step(s, inp):
        d, kv = inp
        return d[..., None] * s + kv, s

    s_final, s_prev = lax.scan(step, s0, (chunk_decay, chunk_kv))
    o_inter = jnp.einsum("bhncd,nbhde->bhnce", q_dec, s_prev)
    o = (o_intra + o_inter).reshape(bsz, nh, n, dv)
    return o, s_final


def hgrn2_mixer(h, w_in, lb_fwd, lb_bwd, gnorm_w, w_out, s0_fwd, s0_bwd):
    bsz, n, _ = h.shape
    hk = HGRN_HEADS * HGRN_DK
    hv = HGRN_HEADS * HGRN_DV
    proj = h @ w_in
    q, f_fwd, f_bwd, v, g = jnp.split(proj, [hk, 2 * hk, 3 * hk, 3 * hk + hv], axis=-1)

    def heads(t, d):
        return t.reshape(bsz, n, HGRN_HEADS, d).transpose(0, 2, 1, 3).astype(jnp.float32)

    qh = jax.nn.silu(heads(q, HGRN_DK))
    vh = heads(v, HGRN_DV)

    def gates(f_logit, lb):
        lb = lb.reshape(HGRN_HEADS, 1, HGRN_DK)
        f = lb + (1.0 - lb) * jax.nn.sigmoid(heads(f_logit, HGRN_DK))
        return 1.0 - f, jnp.log(f)

    k_f, lf_f = gates(f_fwd, lb_fwd)
    k_b, lf_b = gates(f_bwd, lb_bwd)
    o_f, s_f = gla_chunk(qh, k_f, vh, lf_f, s0_fwd.astype(jnp.float32))
    rev = lambda t: jnp.flip(t, axis=2)
    o_b, s_b = gla_chunk(rev(qh), rev(k_b), rev(vh), rev(lf_b), s0_bwd.astype(jnp.float32))
    o = o_f + rev(o_b)
    o = rms_norm(o, gnorm_w) * jax.nn.silu(heads(g, HGRN_DV))
    o = o.transpose(0, 2, 1, 3).reshape(bsz, n, hv).astype(h.dtype)
    return o @ w_out, s_f, s_b


def short_conv_mixer(h, w_in, conv_w, conv_b, w_out, grid_axis):
    gate_b, gate_c, u = jnp.split(h @ w_in, 3, axis=-1)
    z = token_conv(gate_c * u, conv_w, conv_b, grid_axis)
    return (gate_b * z) @ w_out


def conv_ffn(h, w_up, conv_w, conv_b, w_down, grid_axis):
    a, g = jnp.split(h @ w_up, 2, axis=-1)
    a = token_conv(a, conv_w, conv_b, grid_axis)
    return (jax.nn.silu(a) * g) @ w_down


def setup_inputs(seed: int = 0) -> dict:
    key = jax.random.key(seed)
    ks = jax.random.split(key, 22)
    hk = HGRN_HEADS * HGRN_DK
    hv = HGRN_HEADS * HGRN_DV

    def nrm(k, shape, scale):
        return jax.random.normal(k, shape, jnp.float32) * scale

    return {
        "x_prompt": nrm(ks[0], (BATCH, SEQ, D_MODEL), 1.0),
        "x_sample": nrm(ks[1], (DEC_BATCH, DEC_SEQ, D_MODEL), 1.0),
        "state_hgrn": nrm(ks[2], (DEC_BATCH, N_HGRN, 2, HGRN_HEADS, HGRN_DK, HGRN_DV), 0.5),
        "c": nrm(ks[3], (DEC_BATCH, D_MODEL), 1.0),
        "c_ctx": nrm(ks[4], (D_MODEL,), 1.0),
        "ada_w": nrm(ks[5], (DEPTH, D_MODEL, N_MOD * D_MODEL), 0.5 * D_MODEL ** -0.5),
        "ada_b": nrm(ks[6], (DEPTH, N_MOD * D_MODEL), 0.02),
        "norm_mix_w": 1.0 + nrm(ks[7], (DEPTH, D_MODEL), 0.05),
        "norm_ffn_w": 1.0 + nrm(ks[8], (DEPTH, D_MODEL), 0.05),
        "hgrn_w_in": nrm(ks[9], (N_HGRN, D_MODEL, 3 * hk + 2 * hv), D_MODEL ** -0.5),
        "hgrn_lower_bounds": nrm(ks[10], (2, N_HGRN + 1, hk), 0.1),
        "hgrn_gnorm_w": 1.0 + nrm(ks[11], (N_HGRN, HGRN_DV), 0.05),
        "hgrn_w_out": nrm(ks[12], (N_HGRN, hv, D_MODEL), hv ** -0.5),
        "sconv_w_in": nrm(ks[13], (N_CONV, D_MODEL, 3 * D_MODEL), D_MODEL ** -0.5),
        "sconv_conv_w": nrm(ks[14], (N_CONV, CONV_W, D_MODEL), CONV_W ** -0.5),
        "sconv_conv_b": nrm(ks[15], (N_CONV, D_MODEL), 0.01),
        "sconv_w_out": nrm(ks[16], (N_CONV, D_MODEL, D_MODEL), D_MODEL ** -0.5),
        "ffn_w_up": nrm(ks[17], (DEPTH, D_MODEL, 2 * D_FF), D_MODEL ** -0.5),
        "ffn_conv_w": nrm(ks[18], (DEPTH, CONV_W, D_FF), CONV_W ** -0.5),
        "ffn_conv_b": nrm(ks[19], (DEPTH, D_FF), 0.01),
        "ffn_w_down": nrm(ks[20], (DEPTH, D_FF, D_MODEL), D_FF ** -0.5),
        "final_norm_w": 1.0 + nrm(ks[21], (D_MODEL,), 0.05),
    }


def reference(x_prompt, x_sample, state_hgrn, c, c_ctx, ada_w, ada_b, norm_mix_w, norm_ffn_w,
              hgrn_w_in, hgrn_lower_bounds, hgrn_gnorm_w, hgrn_w_out,
              sconv_w_in, sconv_conv_w, sconv_conv_b, sconv_w_out,
              ffn_w_up, ffn_conv_w, ffn_conv_b, ffn_w_down, final_norm_w):
    lb_all = jnp.cumsum(jax.nn.softmax(hgrn_lower_bounds.astype(jnp.float32), axis=1), axis=1)
    zeros_state = jnp.zeros((x_prompt.shape[0], HGRN_HEADS, HGRN_DK, HGRN_DV), jnp.float32)
    xp, xs = x_prompt, x_sample
    ctx_states = []
    for l in range(DEPTH):
        mod_p = jnp.split((jax.nn.silu(c_ctx) @ ada_w[l] + ada_b[l])[None, None, :], N_MOD, axis=-1)
        mod_s = jnp.split((jax.nn.silu(c) @ ada_w[l] + ada_b[l])[:, None, :], N_MOD, axis=-1)
        hp = modulate(xp, norm_mix_w[l], mod_p[0], mod_p[1])
        hs = modulate(xs, norm_mix_w[l], mod_s[0], mod_s[1])
        j = l // N_MIXERS
        if l % N_MIXERS == 0:
            yp, sp_f, sp_b = hgrn2_mixer(hp, hgrn_w_in[j], lb_all[0, j], lb_all[1, j],
                                         hgrn_gnorm_w[j], hgrn_w_out[j], zeros_state, zeros_state)
            ctx_states.append(jnp.stack([sp_f, sp_b], axis=1))
            ys, _, _ = hgrn2_mixer(hs, hgrn_w_in[j], lb_all[0, j], lb_all[1, j],
                                   hgrn_gnorm_w[j], hgrn_w_out[j],
                                   state_hgrn[:, j, 0], state_hgrn[:, j, 1])
        else:
            yp = short_conv_mixer(hp, sconv_w_in[j], sconv_conv_w[j], sconv_conv_b[j],
                                  sconv_w_out[j], None)
            ys = short_conv_mixer(hs, sconv_w_in[j], sconv_conv_w[j], sconv_conv_b[j],
                                  sconv_w_out[j], 1)
        xp = xp + mod_p[2] * yp
        xs = xs + mod_s[2] * ys
        hp = modulate(xp, norm_ffn_w[l], mod_p[3], mod_p[4])
        hs = modulate(xs, norm_ffn_w[l], mod_s[3], mod_s[4])
        xp = xp + mod_p[5] * conv_ffn(hp, ffn_w_up[l], ffn_conv_w[l], ffn_conv_b[l], ffn_w_down[l], None)
        xs = xs + mod_s[5] * conv_ffn(hs, ffn_w_up[l], ffn_conv_w[l], ffn_conv_b[l], ffn_w_down[l], 2)
    y_prompt = rms_norm(xp, final_norm_w)
    y_sample = rms_norm(xs, final_norm_w)
    new_state_hgrn = jnp.stack(ctx_states, axis=1).astype(state_hgrn.dtype)
    return (y_prompt, y_sample, new_state_hgrn)
```

```python
from contextlib import ExitStack
import numpy as np
import concourse.bass as bass
import concourse.mybir as mybir
from concourse.bass_utils import run_bass_kernel_spmd

F32 = mybir.dt.float32
BF16 = mybir.dt.bfloat16
AF = mybir.ActivationFunctionType
ALU = mybir.AluOpType

D = 1024
NPR = 1024
NSX = 1152
NT = NPR + NSX
OWN0 = NPR + 64
NREG = 3072
DFF = 2816
EPS = 1e-6
TILES0 = [(0, 512), (512, 512), (1024, 384), (1408, 384), (1792, 384)]
TILES1 = [(0, 512), (512, 512), (OWN0, 512), (OWN0 + 512, 512)]
FFN_GROUPS = [(0, 8), (8, 7), (15, 7)]
CH = 64
CPT = 128 // CH


class TL:
    def __init__(self, sem, step):
        self.sem = sem
        self.count = 0
        self.step = step


class Builder:
    def __init__(self, nc, es):
        self.nc = nc
        self.es = es
        self.eng = {"pe": nc.tensor, "act": nc.scalar, "dve": nc.vector, "pool": nc.gpsimd, "sp": nc.sync}
        self.tl = {}
        for e in self.eng:
            self.tl[e] = TL(es.enter_context(nc.semaphore("s_" + e)), 1)
        self.seen = {e: {} for e in self.eng}
        self.res = {}

    def timeline(self, name):
        if name not in self.tl:
            self.tl[name] = TL(self.es.enter_context(self.nc.semaphore("d_" + name)), 16)
        return self.tl[name]

    def _waits(self, e, reads, writes):
        need = {}
        for r in reads:
            st = self.res.get(r)
            if st and st["w"]:
                t, v = st["w"]
                need[t] = max(need.get(t, 0), v)
        for w in writes:
            st = self.res.get(w)
            if st:
                if st["w"]:
                    t, v = st["w"]
                    need[t] = max(need.get(t, 0), v)
                for t, v in st["r"].items():
                    need[t] = max(need.get(t, 0), v)
        for t, v in need.items():
            if e == "pe" and t == "pe":
                continue
            if self.tl[t].step == 16:
                v = self.tl[t].count
            if self.seen[e].get(t, 0) < v:
                self.eng[e].wait_ge(self.tl[t].sem, v)
                self.seen[e][t] = v

    def _record(self, t, v, reads, writes):
        for r in reads:
            self.res.setdefault(r, {"w": None, "r": {}})["r"][t] = v
        for w in writes:
            self.res[w] = {"w": (t, v), "r": {}}

    def op(self, e, fn, reads=(), writes=()):
        self._waits(e, reads, writes)
        ins = fn(self.eng[e])
        t = self.tl[e]
        t.count += 1
        ins.then_inc(t.sem, 1)
        self._record(e, t.count, reads, writes)

    def dma(self, e, tlname, out, in_, reads=(), writes=(), **kw):
        self._waits(e, reads, writes)
        t = self.timeline(tlname)
        ins = self.eng[e].dma_start(out=out, in_=in_, **kw)
        t.count += 16
        ins.then_inc(t.sem, 16)
        self._record(tlname, t.count, reads, writes)

    def barrier(self):
        for e in self.eng:
            self.wait_all(e)

    def wait_all(self, e):
        for name, t in self.tl.items():
            if t.count > 0 and self.seen[e].get(name, 0) < t.count:
                self.eng[e].wait_ge(t.sem, t.count)
                self.seen[e][name] = t.count


def tk(name, t0, n):
    return [(name, b) for b in range(t0 // 64, (t0 + n + 63) // 64)]


def build_program(debug=False, upto=99):
    nc = bass.Bass("TRN2", target_bir_lowering=False, dynamic_dma_scratch_size=8192)
    es = ExitStack()
    b = Builder(nc, es)

    def din(name, shape):
        return nc.dram_tensor(name, shape, F32, kind="ExternalInput").ap()

    def dout(name, shape):
        return nc.dram_tensor(name, shape, F32, kind="ExternalOutput").ap()

    xT_p_d = din("xT_p", [D, NPR])
    xT_s_d = din("xT_s", [D, NSX])
    xT_pre_d = din("xT_pre", [D, NREG])
    xT_suf_d = din("xT_suf", [D, NREG])
    mask_d = din("mask_tm", [128, 9 + 24 + 24])
    hmask_d = din("hmask", [128, 2])
    s0_d = din("s0", [128, 2, 8, 128])
    cT_d = din("cT", [128, 8, 2])
    ada_w_d = din("ada_w", [2, D, 6 * D])
    ada_b_d = din("ada_b", [128, 2, 48])
    nmix_d = din("nmix", [128, 2, 8])
    nffn_d = din("nffn", [128, 2, 8])
    nfin_d = din("nfin", [128, 8])
    hw_in_d = din("hw_in", [8, D, 640])
    lbraw_d = din("lbraw", [2, 2, D])
    gnw_d = din("gnw", [128, 1])
    hw_out_d = din("hw_out", [D, D])
    sw_in_d = din("sw_in", [D, 3 * D])
    scw_d = din("scw", [128, 3, 8])
    scb_d = din("scb", [128, 8])
    sw_out_d = din("sw_out", [D, D])
    fw_up_d = din("fw_up", [2, D, 2 * DFF])
    fcw_d = din("fcw", [128, 2, 3, 22])
    fcb_d = din("fcb", [128, 2, 22])
    fw_dn_d = din("fw_dn", [2, DFF, D])
    yT_p_d = dout("yT_p", [D, NPR])
    yT_s_d = dout("yT_s", [D, NPR])
    ns_d = dout("ns", [4, 2, 8, 128, 128])
    dbg_d = dout("dbg", [4, D, NT]) if debug else None

    def sb(name, shape, dt):
        return es.enter_context(nc.sbuf_tensor(name, shape, dt))

    xT = sb("xT", [128, 8, NT], F32)[:]
    actB = sb("actB", [128, 8 * NT], BF16)[:]
    oT = actB.rearrange("p (k n) -> p k n", k=8)
    xstage = actB.bitcast(F32)[:, 0:8 * 1024].rearrange("p (k n) -> p k n", k=8)
    ARENA_F32 = 17408
    arena = sb("arena", [128, ARENA_F32], F32)[:]
    arena_bf = arena.bitcast(BF16)

    class Carve:
        def __init__(self, base_f32=None, base_bf=None, limit=None):
            self.off = 0
            self.bf32 = arena if base_f32 is None else base_f32
            self.bbf = arena_bf if base_bf is None else base_bf
            self.limit = ARENA_F32 * 4 if limit is None else limit

        def take(self, shape, dt):
            n = int(np.prod(shape))
            esz = 4 if dt == F32 else 2
            self.off = (self.off + 31) // 32 * 32
            o = self.off
            self.off += n * esz
            assert self.off <= self.limit, ("arena overflow", self.off)
            base = self.bf32 if dt == F32 else self.bbf
            v = base[:, o // esz:o // esz + n]
            if len(shape) == 2:
                return v.rearrange("p (a b) -> p a b", a=shape[0])
            if len(shape) == 3:
                return v.rearrange("p (a b c) -> p a b c", a=shape[0], b=shape[1])
            return v

    NW = 2
    wslot = [sb("wslot%d" % i, [128, 8, 640], BF16)[:] for i in range(NW)]
    ps = [es.enter_context(nc.psum_tensor("ps%d" % i, [128, 512], F32))[:] for i in range(8)]

    ones_bf = sb("ones_bf", [128, 128], BF16)[:]
    ones_f = sb("ones_f", [128, 128], F32)[:]
    ident = sb("ident", [128, 128], F32)[:]
    tri = sb("tri", [128, 6, 128], F32)[:]
    iop = sb("iop", [128, 1], F32)[:]
    modv = sb("modv", [128, 2, 48, 2], F32)[:]
    adab = sb("adab", [128, 2, 48], F32)[:]
    nmix = sb("nmix_s", [128, 2, 8], F32)[:]
    nffn = sb("nffn_s", [128, 2, 8], F32)[:]
    nfin = sb("nfin_s", [128, 8], F32)[:]
    Amod = sb("Amod", [128, 5, 8, 2], F32)[:]
    cT = sb("cT_s", [128, 8, 2], F32)[:]
    scT = sb("scT", [128, 8, 2], BF16)[:]
    oml = sb("oml", [128, 8, 2, 128], F32)[:]
    gnw = sb("gnw_s", [128, 1], F32)[:]
    scw = sb("scw_s", [128, 3, 8], F32)[:]
    scb = sb("scb_s", [128, 8], F32)[:]
    fcw = sb("fcw_s", [128, 2, 3, 22], F32)[:]
    fcb = sb("fcb_s", [128, 2, 22], F32)[:]
    maskt = sb("maskt", [128, 57], F32)[:]
    hmask = sb("hmask_s", [128, 2], F32)[:]
    Sinit = sb("Sinit", [128, 2, 8, 128], F32)[:]
    epsc = sb("epsc", [128, 1], F32)[:]

    cnt = {"w": 0, "ps": 0, "q": 0}

    pools = {"big": [0, 1, 2, 3, 4, 5, 6, 7], "small": [4, 5, 6, 7]}

    def bank():
        i = pools["big"][cnt["ps"] % len(pools["big"])]
        cnt["ps"] += 1
        return ps[i], ("ps", i)

    def qslot():
        i = pools["small"][cnt["q"] % len(pools["small"])]
        cnt["q"] += 1
        return ps[i][:, 0:128], ("ps", i)

    def qslot4():
        i = pools["small"][cnt["q"] % len(pools["small"])]
        cnt["q"] += 1
        return ps[i], ("ps", i)

    def wload(src3, ncols, reads=()):
        i = cnt["w"] % NW
        cnt["w"] += 1
        K = src3.shape[1]
        dst = wslot[i].rearrange("p k n -> p (k n)")[:, 0:K * ncols].rearrange("p (k n) -> p k n", k=K)
        b.dma("pool", "w%d" % i, dst, src3, reads=list(reads), writes=[("w", i)])
        return dst, ("w", i)

    def wview(w2d, c0, ncols):
        return w2d.rearrange("(k p) n -> p k n", p=128)[:, :, c0:c0 + ncols]

    def mm(out, lhsT, rhs, start, stop, reads, writes, **kw):
        b.op("pe", lambda e: e.matmul(out, lhsT=lhsT, rhs=rhs, start=start, stop=stop, **kw), reads, writes)

    def act(out, in_, func, reads, writes, **kw):
        b.op("act", lambda e: e.activation(out=out, in_=in_, func=func, **kw), reads, writes)

    def tt(out, in0, in1, op, reads, writes, e="dve"):
        b.op(e, lambda g: g.tensor_tensor(out=out, in0=in0, in1=in1, op=op), reads, writes)

    def ts(out, in0, s1, s2, op0, op1, reads, writes, e="dve"):
        if s2 is None:
            b.op(e, lambda g: g.tensor_scalar(out=out, in0=in0, scalar1=s1, scalar2=None, op0=op0), reads, writes)
        else:
            b.op(e, lambda g: g.tensor_scalar(out=out, in0=in0, scalar1=s1, scalar2=s2, op0=op0, op1=op1), reads, writes)

    def stt(out, in0, scalar, in1, op0, op1, reads, writes):
        b.op("dve", lambda g: g.scalar_tensor_tensor(out=out, in0=in0, scalar=scalar, in1=in1, op0=op0, op1=op1),
             reads, writes)

    def cp(out, in_, reads, writes, e="dve"):
        b.op(e, lambda g: g.tensor_copy(out=out, in_=in_), reads, writes)

    def ld(dst, src, key):
        b.dma("sp", "c_" + str(key), dst, src, writes=[key])

    cv0 = Carve()
    lbt = cv0.take([2, 2, D], F32)
    iot = cv0.take([1, 128], F32)[:, 0, :]
    tmpc = cv0.take([4, 128], F32)
    pc = cv0.take([1, 2], F32)[:, 0, :]
    sinit_keys = [("Sinit", dr, h) for dr in range(2) for h in range(8)]
    for dst, src, key in [(adab, ada_b_d, ["adab"]), (nmix, nmix_d, ["nmix"]), (nffn, nffn_d, ["nffn"]),
                          (nfin, nfin_d, ["nfin"]), (cT, cT_d, ["cT"]), (gnw, gnw_d, ["gnw"]), (scw, scw_d, ["scw"]),
                          (scb, scb_d, ["scb"]), (fcw, fcw_d, ["fcw"]), (fcb, fcb_d, ["fcb"]),
                          (maskt, mask_d, ["maskt"]), (hmask, hmask_d, ["hmask"]), (Sinit, s0_d, sinit_keys)]:
        b.dma("sp", "cst", dst, src, writes=key)
    b.dma("sp", "cst", lbt.rearrange("p a b d -> p (a b d)"),
          lbraw_d.rearrange("a b d -> (a b d)").partition_broadcast(128), writes=["lbt"])

    b.op("pool", lambda g: g.memset(ones_bf, 1.0), [], ["ones_bf"])
    b.op("pool", lambda g: g.memset(ones_f, 1.0), [], ["ones_f"])
    b.op("pool", lambda g: g.memset(epsc, EPS), [], ["epsc"])
    b.op("pool", lambda g: g.iota(iot, pattern=[[1, 128]], base=0, channel_multiplier=0,
                                  allow_small_or_imprecise_dtypes=True), [], ["iot"])
    b.op("pool", lambda g: g.iota(iop, pattern=[[0, 1]], base=0, channel_multiplier=1,
                                  allow_small_or_imprecise_dtypes=True), [], ["iop"])
    b.op("pool", lambda g: g.iota(tmpc[:, 1, :], pattern=[[1, 128 // CH], [0, CH]], base=0, channel_multiplier=0,
                                  allow_small_or_imprecise_dtypes=True), [], ["tc1"])
    ts(pc[:, 0:1], iop, float(CH), None, ALU.is_ge, None, ["iop"], ["pc0"])
    for thr in range(2 * CH, 128, CH):
        ts(pc[:, 1:2], iop, float(thr), None, ALU.is_ge, None, ["iop", "pc0"], ["pc1"])
        tt(pc[:, 0:1], pc[:, 0:1], pc[:, 1:2], ALU.add, ["pc0", "pc1"], ["pc0"])
    ts(tmpc[:, 0, :], iot, iop[:, 0:1], None, ALU.subtract, None, ["iot", "iop"], ["tc0"])
    ts(tmpc[:, 3, :], tmpc[:, 1, :], pc[:, 0:1], None, ALU.is_equal, None, ["tc1", "pc0"], ["tc3"])
    dmp = tmpc[:, 0, :]
    same = tmpc[:, 3, :]
    ts(ident, dmp, 0.0, None, ALU.is_equal, None, ["tc0"], ["ident"])
    for i, (opc, thr) in enumerate([(ALU.is_ge, 0.0), (ALU.is_le, 0.0), (ALU.is_lt, 0.0), (ALU.is_gt, 0.0)]):
        ts(tri[:, i, :], dmp, thr, None, opc, None, ["tc0"], [("tri", i)])
        tt(tri[:, i, :], tri[:, i, :], same, ALU.mult, [("tri", i), "tc3"], [("tri", i)])
    ts(tri[:, 4, :], dmp, 0.0, None, ALU.is_lt, None, ["tc0"], [("tri", 4)])
    ts(tri[:, 5, :], dmp, 0.0, None, ALU.is_gt, None, ["tc0"], [("tri", 5)])

    tt(lbt[:, :, 0, :], lbt[:, :, 1, :], lbt[:, :, 0, :], ALU.subtract, ["lbt"], ["lbt"])
    for dr in range(2):
        act(oml[:, :, dr, :], lbt[:, dr, 0, :].rearrange("p (h d) -> p h d", h=8), AF.Sigmoid, ["lbt"], ["oml"])

    act(scT.rearrange("p k v -> p (k v)"), cT.rearrange("p k v -> p (k v)"), AF.Silu, ["cT"], ["scT"])
    def ada_group(l, g):
        w, wk = wload(wview(ada_w_d[l], g * 512, 512), 512)
        pst, pk = bank()
        for m4 in range(4):
            for k in range(8):
                mm(pst[:, m4 * 2:m4 * 2 + 2], w[:, k, m4 * 128:(m4 + 1) * 128], scT[:, k, :], k == 0, k == 7,
                   [wk, "scT"], [pk])
        for v in range(2):
            tt(modv[:, l, g * 4:(g + 1) * 4, v], pst[:, 0:8].rearrange("p (m v) -> p m v", v=2)[:, :, v],
               adab[:, l, g * 4:(g + 1) * 4], ALU.add, [pk, "adab"], ["modv"])

    def ada_finish(l):
        for i in (1, 4):
            ts(modv[:, l, i * 8:(i + 1) * 8, :], modv[:, l, i * 8:(i + 1) * 8, :], 1.0, None, ALU.add, None,
               ["modv"], ["modv"])
        for v in range(2):
            tt(Amod[:, 2 * l, :, v], nmix[:, l, :], modv[:, l, 8:16, v], ALU.mult, ["modv", "nmix"], ["Amod"])
            tt(Amod[:, 2 * l + 1, :, v], nffn[:, l, :], modv[:, l, 32:40, v], ALU.mult, ["modv", "nffn"], ["Amod"])

    for g in range(12):
        ada_group(0, g)
    ada_finish(0)
    ada_pending = [(1, g) for g in range(12)]
    ada_done = []
    for v in range(2):
        cp(Amod[:, 4, :, v], nfin, ["nfin"], ["Amod"])

    for k in range(8):
        b.dma("sp", "xin", xT[:, k, 0:NPR], xT_p_d[k * 128:(k + 1) * 128, :], writes=tk("xT", 0, NPR))
        b.dma("sp", "xin", xT[:, k, NPR:NT], xT_s_d[k * 128:(k + 1) * 128, :], writes=tk("xT", NPR, NSX))

    def norm_mod(cv, tiles, ai, shift_i, l, src, srck, dst, dstk, vsel=None, final_out=None):
        sq = cv.take([8, 512], BF16)
        rs = cv.take([1, 512], F32)[:, 0, :]
        tmp = [cv.take([1, 512], F32)[:, 0, :] for _ in range(2)]
        for ti, (t0, n) in enumerate(tiles):
            v = (0 if t0 < NPR else 1) if vsel is None else vsel
            rk = tk(srck, t0, n)
            act(sq[:, :, 0:n], src[:, :, t0:t0 + n], AF.Square, rk, ["nm_sq"])
            pst, pk = bank()
            for k in range(8):
                mm(pst[:, 0:n], ones_bf, sq[:, k, 0:n], k == 0, k == 7, ["nm_sq", "ones_bf"], [pk])
            act(rs[:, 0:n], pst[:, 0:n], AF.Ln, [pk, "epsc"], ["nm_rs"], scale=1.0 / D, bias=epsc[:, 0:1])
            act(rs[:, 0:n], rs[:, 0:n], AF.Exp, ["nm_rs"], ["nm_rs"], scale=-0.5)
            for k in range(8):
                if final_out is not None:
                    stt(final_out[:, k, t0:t0 + n], src[:, k, t0:t0 + n], Amod[:, ai, k, v:v + 1], rs[:, 0:n],
                        ALU.mult, ALU.mult, rk + ["nm_rs", "Amod"], tk(dstk, t0, n))
                    continue
                tb = tmp[k % 2]
                stt(tb[:, 0:n], src[:, k, t0:t0 + n], Amod[:, ai, k, v:v + 1], rs[:, 0:n], ALU.mult, ALU.mult,
                    rk + ["nm_rs", "Amod"], [("nm_t", k % 2)])
                act(dst[:, k, t0:t0 + n], tb[:, 0:n], AF.Identity, [("nm_t", k % 2), "modv"], tk(dstk, t0, n),
                    bias=modv[:, l, shift_i * 8 + k, v:v + 1], scale=1.0)

    def out_proj(w2d, src, srck, tiles, l, gate_i, nk=8):
        for g in range(2):
            w, wk = wload(wview(w2d, g * 512, 512), 512)
            for m4 in range(4):
                m = g * 4 + m4
                for (t0, n) in tiles:
                    v = 0 if t0 < NPR else 1
                    pst, pk = bank()
                    for k in range(nk):
                        mm(pst[:, 0:n], w[:, k, m4 * 128:(m4 + 1) * 128], src[:, k, t0:t0 + n], k == 0, k == nk - 1,
                           [wk] + tk(srck, t0, n), [pk])
                    stt(xT[:, m, t0:t0 + n], pst[:, 0:n], modv[:, l, gate_i * 8 + m, v:v + 1], xT[:, m, t0:t0 + n],
                        ALU.mult, ALU.add, [pk, "modv"] + tk("xT", t0, n), tk("xT", t0, n))

    def conv_ffn(l, tiles):
        b.barrier()
        cv = Carve()
        hT = cv.take([8, NT], BF16)
        norm_mod(cv, tiles, 2 * l + 1, 3, l, xT, "xT", hT, "hT")
        t1 = [cv.take([1, 512], F32)[:, 0, :] for _ in range(2)]
        s1 = [cv.take([1, 512], F32)[:, 0, :] for _ in range(2)]
        mT = oT
        wup = fw_up_d[l]
        for (c0, G) in FFN_GROUPS:
            for ci in range(G):
                c = c0 + ci
                i = cnt["w"] % NW
                cnt["w"] += 1
                w = wslot[i][:, :, 0:256]
                b.dma("pool", "w%d" % i, w[:, :, 0:128], wview(wup, c * 128, 128), writes=[("w", i)])
                b.dma("pool", "w%d" % i, w[:, :, 128:256], wview(wup, DFF + c * 128, 128), writes=[("w", i)])
                wk = ("w", i)
                for ti, (t0, n) in enumerate(tiles):
                    seg = 256 if t0 < NPR else 64
                    nsg = n // seg
                    pa, pak = bank()
                    pg, pgk = bank()
                    hk = tk("hT", t0, n)
                    for k in range(8):
                        mm(pa[:, 0:n], w[:, k, 0:128], hT[:, k, t0:t0 + n], k == 0, k == 7, [wk] + hk, [pak])
                    for k in range(8):
                        mm(pg[:, 0:n], w[:, k, 128:256], hT[:, k, t0:t0 + n], k == 0, k == 7, [wk] + hk, [pgk])
                    j = ti % 2
                    a1 = t1[j]
                    act(a1[:, 0:n], pa[:, 0:n], AF.Identity, [pak, "fcw", "fcb"], [("ff_t", j)],
                        scale=fcw[:, l, 1, c:c + 1], bias=fcb[:, l, c:c + 1])
                    a3 = a1[:, 0:n].rearrange("p (s t) -> p s t", t=seg)
                    p3 = pa[:, 0:n].rearrange("p (s t) -> p s t", t=seg)
                    stt(a3[:, :, 1:seg], p3[:, :, 0:seg - 1], fcw[:, l, 0, c:c + 1], a3[:, :, 1:seg], ALU.mult, ALU.add,
                        [pak, ("ff_t", j), "fcw"], [("ff_t", j)])
                    stt(a3[:, :, 0:seg - 1], p3[:, :, 1:seg], fcw[:, l, 2, c:c + 1], a3[:, :, 0:seg - 1], ALU.mult,
                        ALU.add, [pak, ("ff_t", j), "fcw"], [("ff_t", j)])
                    act(s1[j][:, 0:n], a1[:, 0:n], AF.Silu, [("ff_t", j)], [("ff_s", j)])
                    tt(mT[:, ci, t0:t0 + n], pg[:, 0:n], s1[j][:, 0:n], ALU.mult, [pgk, ("ff_s", j)],
                       tk(("mT", ci), t0, n))
                if ada_pending:
                    ada_group(*ada_pending.pop(0))
            if ada_pending and c0 + G >= 22:
                while ada_pending:
                    ada_group(*ada_pending.pop(0))
            if l == 0 and c0 + G >= 22 and not ada_done:
                ada_finish(1)
                ada_done.append(1)
            wdn = fw_dn_d[l]
            for g in range(2):
                wsrc = wdn.rearrange("(k p) n -> p k n", p=128)[:, c0:c0 + G, g * 512:(g + 1) * 512]
                w, wk = wload(wsrc, 512)
                for m4 in range(4):
                    m = g * 4 + m4
                    for (t0, n) in tiles:
                        v = 0 if t0 < NPR else 1
                        pst, pk = bank()
                        for ci in range(G):
                            mm(pst[:, 0:n], w[:, ci, m4 * 128:(m4 + 1) * 128], mT[:, ci, t0:t0 + n], ci == 0,
                               ci == G - 1, [wk] + tk(("mT", ci), t0, n), [pk])
                        stt(xT[:, m, t0:t0 + n], pst[:, 0:n], modv[:, l, 40 + m, v:v + 1], xT[:, m, t0:t0 + n],
                            ALU.mult, ALU.add, [pk, "modv"] + tk("xT", t0, n), tk("xT", t0, n))

    def sconv(l):
        b.barrier()
        cv = Carve()
        hT = cv.take([8, NT], BF16)
        cvn = Carve()
        cvn.off = cv.off
        norm_mod(cvn, TILES0, 2 * l, 0, l, xT, "xT", hT, "hT")
        b.barrier()
        cu = cv.take([1, NT], F32)[:, 0, :]
        gb = cv.take([1, NT], BF16)[:, 0, :]
        ub1 = cv.take([1, 512], F32)[:, 0, :]
        ub = [ub1, ub1]
        zb = cv.take([1, NT], F32)[:, 0, :]
        pT = oT
        for c in range(8):
            i = cnt["w"] % NW
            cnt["w"] += 1
            w = wslot[i][:, :, 0:384]
            for q in range(3):
                b.dma("pool", "w%d" % i, w[:, :, q * 128:(q + 1) * 128], wview(sw_in_d, q * D + c * 128, 128),
                      writes=[("w", i)])
            wk = ("w", i)
            for ti, (t0, n) in enumerate(TILES0):
                hk = tk("hT", t0, n)
                pp = []
                for q in range(3):
                    pst, pk = bank()
                    for k in range(8):
                        mm(pst[:, 0:n], w[:, k, q * 128:(q + 1) * 128], hT[:, k, t0:t0 + n], k == 0, k == 7, [wk] + hk,
                           [pk])
                    pp.append((pst, pk))
                j = 0
                act(gb[:, t0:t0 + n], pp[0][0][:, 0:n], AF.Copy, [pp[0][1]], tk("sc_gb", t0, n))
                act(ub[j][:, 0:n], pp[2][0][:, 0:n], AF.Copy, [pp[2][1]], [("sc_u", j)])
                tt(cu[:, t0:t0 + n], pp[1][0][:, 0:n], ub[j][:, 0:n], ALU.mult, [pp[1][1], ("sc_u", j)],
                   tk("sc_cu", t0, n))
            ts(cu[:, NPR:NPR + 64], cu[:, NPR:NPR + 64], hmask[:, 0:1], None, ALU.mult, None,
               tk("sc_cu", NPR, 64) + ["hmask"], tk("sc_cu", NPR, 64))
            ts(cu[:, NT - 64:NT], cu[:, NT - 64:NT], hmask[:, 1:2], None, ALU.mult, None,
               tk("sc_cu", NT - 64, 64) + ["hmask"], tk("sc_cu", NT - 64, 64))
            rk = tk("sc_cu", 0, NPR)
            zk = tk("sc_z", 0, NPR)
            act(zb[:, 0:NPR], cu[:, 0:NPR], AF.Identity, rk + ["scw", "scb"], zk, scale=scw[:, 1, c:c + 1],
                bias=scb[:, c:c + 1])
            z3 = zb[:, 0:NPR].rearrange("p (s t) -> p s t", t=256)
            c3 = cu[:, 0:NPR].rearrange("p (s t) -> p s t", t=256)
            stt(z3[:, :, 1:256], c3[:, :, 0:255], scw[:, 0, c:c + 1], z3[:, :, 1:256], ALU.mult, ALU.add, rk + zk, zk)
            stt(z3[:, :, 0:255], c3[:, :, 1:256], scw[:, 2, c:c + 1], z3[:, :, 0:255], ALU.mult, ALU.add, rk + zk, zk)
            tt(pT[:, c, 0:NPR], zb[:, 0:NPR], gb[:, 0:NPR], ALU.mult, zk + tk("sc_gb", 0, NPR), tk("pT", 0, NPR))
            rk = tk("sc_cu", NPR, NSX)
            zk = tk("sc_z", OWN0, 1024)
            act(zb[:, OWN0:OWN0 + 1024], cu[:, OWN0:OWN0 + 1024], AF.Identity, rk + ["scw", "scb"], zk,
                scale=scw[:, 1, c:c + 1], bias=scb[:, c:c + 1])
            stt(zb[:, OWN0:OWN0 + 1024], cu[:, OWN0 - 64:OWN0 + 960], scw[:, 0, c:c + 1], zb[:, OWN0:OWN0 + 1024],
                ALU.mult, ALU.add, rk + zk, zk)
            stt(zb[:, OWN0:OWN0 + 1024], cu[:, OWN0 + 64:OWN0 + 1088], scw[:, 2, c:c + 1], zb[:, OWN0:OWN0 + 1024],
                ALU.mult, ALU.add, rk + zk, zk)
            tt(pT[:, c, OWN0:OWN0 + 1024], zb[:, OWN0:OWN0 + 1024], gb[:, OWN0:OWN0 + 1024], ALU.mult,
               zk + tk("sc_gb", OWN0, 1024), tk("pT", OWN0, 1024))
        out_proj(sw_out_d, pT, "pT", TILES1, l, 2)

    def hgrn_summary():
        b.barrier()
        pools["big"] = [0, 1, 2, 3]
        pools["small"] = [4, 5, 6, 7]
        cv = Carve()
        hb = cv.take([8, 1024], BF16)
        Wv = cv.take([8, 1024], BF16)
        cvn0 = cv.off
        sq = cv.take([8, 256], BF16)
        rs = cv.take([1, 256], F32)[:, 0, :]
        tmpn = [cv.take([1, 256], F32)[:, 0, :] for _ in range(2)]
        NB = 3
        ez = [cv.take([1, 512], F32)[:, 0, :] for _ in range(NB)]
        lf = [cv.take([1, 512], F32)[:, 0, :] for _ in range(NB)]
        eb = [cv.take([1, 512], F32)[:, 0, :] for _ in range(NB)]
        ksb = [cv.take([1, 512], BF16)[:, 0, :] for _ in range(NB)]
        vt = [cv.take([1, 512], BF16)[:, 0, :] for _ in range(NB)]
        dd = [cv.take([1, 4], F32)[:, 0, :] for _ in range(NB)]
        Wf = [wslot[i].rearrange("p k n -> p (k n)")[:, 0:4096].rearrange("p (k n) -> p k n", k=4) for i in range(2)]
        for h in range(8):
            wsrc = hw_in_d[h].rearrange("(k p) n -> p k n", p=128)
            b.dma("pool", "wv", Wv[:, :, h * 128:(h + 1) * 128], wsrc[:, :, 256:384], writes=["Wv"])
        for dr, xd, moff in ((0, xT_pre_d, 9), (1, xT_suf_d, 33)):
            for h in range(8):
                wsrc = hw_in_d[h].rearrange("(k p) n -> p k n", p=128)
                for i in range(2):
                    b.dma("pool", "w%d" % i, Wf[i][:, :, h * 128:(h + 1) * 128],
                          wsrc[:, i * 4:(i + 1) * 4, dr * 128:(dr + 1) * 128], writes=[("w", i)])
            blocks = [0, 1, 2] if dr == 0 else [2, 1, 0]
            for blk in blocks:
                for k in range(8):
                    b.dma("sp", "xst", xstage[:, k, :], xd[k * 128:(k + 1) * 128, blk * 1024:(blk + 1) * 1024],
                          writes=tk("xst", 0, 1024))
                for t0 in range(0, 1024, 256):
                    n = 256
                    rk = tk("xst", t0, n)
                    act(sq[:, :, 0:n], xstage[:, :, t0:t0 + n], AF.Square, rk, ["nm_sq"])
                    pst, pk = bank()
                    for k in range(8):
                        mm(pst[:, 0:n], ones_bf, sq[:, k, 0:n], k == 0, k == 7, ["nm_sq", "ones_bf"], [pk])
                    act(rs[:, 0:n], pst[:, 0:n], AF.Ln, [pk, "epsc"], ["nm_rs"], scale=1.0 / D, bias=epsc[:, 0:1])
                    act(rs[:, 0:n], rs[:, 0:n], AF.Exp, ["nm_rs"], ["nm_rs"], scale=-0.5)
                    for k in range(8):
                        tb = tmpn[k % 2]
                        stt(tb[:, 0:n], xstage[:, k, t0:t0 + n], Amod[:, 0, k, 1:2], rs[:, 0:n], ALU.mult, ALU.mult,
                            rk + ["nm_rs", "Amod"], [("nm_t", k % 2)])
                        act(hb[:, k, t0:t0 + n], tb[:, 0:n], AF.Identity, [("nm_t", k % 2), "modv"],
                            tk("hb", t0, n), bias=modv[:, 0, k, 1:2], scale=1.0)
                tiles_ = list(range(8)) if dr == 0 else list(range(7, -1, -1))
                units = [(tix, hh) for tix in tiles_ for hh in range(2)]
                st = {}

                def S1(u):
                    tix, hh = units[u]
                    j = u % NB
                    t0 = tix * 128
                    c0 = hh * 512
                    pf, pfk = bank()
                    for k in range(8):
                        mm(pf, hb[:, k, t0:t0 + 128], Wf[k // 4][:, k % 4, c0:c0 + 512], k == 0, k == 7,
                           [("w", k // 4)] + tk("hb", t0, 128), [pfk])
                    pv, pvk = bank()
                    for k in range(8):
                        mm(pv, hb[:, k, t0:t0 + 128], Wv[:, k, c0:c0 + 512], k == 0, k == 7,
                           ["Wv"] + tk("hb", t0, 128), [pvk])
                    act(ez[j], pf, AF.Exp, [pfk], [("ez", j)])
                    cp(vt[j], pv, [pvk], [("vt", j)])
                    act(ez[j], ez[j], AF.Ln, [("ez", j), "ones_f"], [("ez", j)], bias=ones_f[:, 0:1], scale=1.0)
                    act(ez[j], ez[j], AF.Exp, [("ez", j)], [("ez", j)], scale=-1.0)
                    tcol = blk * 8 + tix
                    stt(ez[j].rearrange("p (h d) -> p h d", h=4), ez[j].rearrange("p (h d) -> p h d", h=4),
                        maskt[:, moff + tcol:moff + tcol + 1], oml[:, hh * 4:(hh + 1) * 4, dr, :], ALU.mult, ALU.mult,
                        [("ez", j), "maskt", "oml"], [("ez", j)])
                    act(lf[j], ez[j], AF.Ln, [("ez", j), "ones_f"], [("lf", j)], scale=-1.0, bias=ones_f[:, 0:1])

                def S2(u):
                    tix, hh = units[u]
                    j = u % NB
                    pB, pBk = qslot4()
                    mm(pB, tri[:, 4 + dr, :], lf[j], True, True, [("tri", 4 + dr), ("lf", j)], [pBk])
                    pD, pDk = qslot4()
                    for h4 in range(4):
                        mm(pD[:, h4 * 128:h4 * 128 + 2], lf[j][:, h4 * 128:(h4 + 1) * 128], ones_f[:, 0:2], True, True,
                           [("lf", j), "ones_f"], [pDk])
                    act(eb[j], pB, AF.Exp, [pBk], [("eb", j)])
                    act(dd[j], pD.rearrange("p (h d) -> p h d", h=4)[:, :, 0], AF.Exp, [pDk], [("dd", j)])
                    tt(ksb[j], ez[j], eb[j], ALU.mult, [("ez", j), ("eb", j)], [("ksb", j)])

                def S3(u):
                    tix, hh = units[u]
                    j = u % NB
                    pKV, pKVk = qslot4()
                    for h4 in range(4):
                        mm(pKV[:, h4 * 128:(h4 + 1) * 128], ksb[j][:, h4 * 128:(h4 + 1) * 128],
                           vt[j][:, h4 * 128:(h4 + 1) * 128], True, True, [("ksb", j), ("vt", j)], [pKVk])
                    for h4 in range(4):
                        h = hh * 4 + h4
                        stt(Sinit[:, dr, h, :], Sinit[:, dr, h, :], dd[j][:, h4:h4 + 1],
                            pKV[:, h4 * 128:(h4 + 1) * 128], ALU.mult, ALU.add,
                            [pKVk, ("dd", j), ("Sinit", dr, h)], [("Sinit", dr, h)])

                nu = len(units)
                for i in range(nu + 2):
                    if i < nu:
                        S1(i)
                    if 0 <= i - 1 < nu:
                        S2(i - 1)
                    if 0 <= i - 2 < nu:
                        S3(i - 2)

    def hgrn_main(l):
        regions = [
            dict(t0=0, ntile=8, seqs=[(0, 2), (2, 2), (4, 2), (6, 2)], masked=False, init=False, out_state=True),
            dict(t0=NPR, ntile=9, seqs=[(0, 9)], masked=True, init=True, out_state=False),
        ]
        for R in regions:
            b.barrier()
            cv = Carve()
            nt_ = R["ntile"]
            T0 = R["t0"]
            ntok = nt_ * 128
            hT = cv.take([8, ntok], BF16)
            rtiles = [(T0 + a_, n_) for (a_, n_) in
                      ([(0, 512), (512, 512)] if T0 == 0 else [(0, 384), (384, 384), (768, 384)])]
            cvn = Carve()
            cvn.off = cv.off
            sq = cvn.take([8, 512], BF16)
            rs = cvn.take([1, 512], F32)[:, 0, :]
            tmpb = [cvn.take([1, 512], F32)[:, 0, :] for _ in range(2)]
            for (t0, n) in rtiles:
                v = 0 if t0 < NPR else 1
                rk = tk("xT", t0, n)
                act(sq[:, :, 0:n], xT[:, :, t0:t0 + n], AF.Square, rk, ["nm_sq"])
                pst, pk = bank()
                for k in range(8):
                    mm(pst[:, 0:n], ones_bf, sq[:, k, 0:n], k == 0, k == 7, ["nm_sq", "ones_bf"], [pk])
                act(rs[:, 0:n], pst[:, 0:n], AF.Ln, [pk, "epsc"], ["nm_rs"], scale=1.0 / D, bias=epsc[:, 0:1])
                act(rs[:, 0:n], rs[:, 0:n], AF.Exp, ["nm_rs"], ["nm_rs"], scale=-0.5)
                for k in range(8):
                    tb = tmpb[k % 2]
                    stt(tb[:, 0:n], xT[:, k, t0:t0 + n], Amod[:, 2 * l, k, v:v + 1], rs[:, 0:n], ALU.mult, ALU.mult,
                        rk + ["nm_rs", "Amod"], [("nm_t", k % 2)])
                    act(hT[:, k, t0 - T0:t0 - T0 + n], tb[:, 0:n], AF.Identity, [("nm_t", k % 2), "modv"],
                        tk("hTr", t0 - T0, n), bias=modv[:, l, k, v:v + 1], scale=1.0)
            b.barrier()
            pools["big"] = [0, 1]
            pools["small"] = [2, 3, 4, 5]
            cvb = Carve(base_f32=actB.bitcast(F32), base_bf=actB, limit=8 * NT * 2)
            P = []
            for i2 in range(2):
                P.append(dict(qd=cvb.take([2, NSX], BF16), scm=cvb.take([2, 9, 128], BF16),
                              vtm=cvb.take([9, 128], BF16), ksm=cvb.take([2, 9, 128], BF16),
                              gs=cv.take([1, NSX], BF16)[:, 0, :], Dc=cv.take([2, 36], F32),
                              oTh=cv.take([1, NSX], BF16)[:, 0, :], wo=cv.take([1, D], BF16)[:, 0, :]))
            qs = cv.take([1, NSX], BF16)[:, 0, :]
            SprevB = cv.take([9 * CPT, 128], BF16)
            RING = 2 * CPT
            NSF = 2 * RING if len(R["seqs"]) > 1 else RING
            SprevF = cv.take([NSF, 128], BF16)
            Sst = cv.take([6, 128], F32)
            NB = 2
            ki = [cv.take([2, 128], BF16) for _ in range(2)]
            kt = [cv.take([1, 256], F32)[:, 0, :] for _ in range(NB)]
            lf = [cv.take([1, 256], F32)[:, 0, :] for _ in range(NB)]
            eg = [cv.take([2, 128], F32) for _ in range(2)]
            en = [cv.take([2, 128], F32) for _ in range(2)]
            ebb = [cv.take([2, 128], F32) for _ in range(2)]
            osq = [cv.take([1, 128], BF16)[:, 0, :] for _ in range(2)]
            rso = [cv.take([1, 128], F32)[:, 0, :] for _ in range(2)]
            o1 = cv.take([1, 128], F32)[:, 0, :]
            ocnt = [0]
            wcache = {}

            def get_w(h):
                if h not in wcache and h < 8:
                    wsrc = hw_in_d[h].rearrange("(k p) n -> p k n", p=128)
                    wcache[h] = wload(wsrc, 640)
                return wcache.get(h)

            def make_head(h):
                pb = P[h % 2]
                hp = h % 2
                qd, scm, vtm, ksm, gs, Dc, oTh, wo = (pb[k_] for k_ in ("qd", "scm", "vtm", "ksm", "gs", "Dc", "oTh", "wo"))
                w, wk = get_w(h)

                def A():
                    b.dma("pool", "wo%d" % hp, wo, hw_out_d[h * 128:(h + 1) * 128, :], writes=[("wo", hp)])
                    get_w(h + 1)
                    pools["big"] = [0, 1, 2, 3, 4, 5]
                    for (t0, n) in rtiles:
                        lt = t0 - T0
                        hk = tk("hTr", lt, n)
                        pq, pqk = bank()
                        for k in range(8):
                            mm(pq[:, 0:n], w[:, k, 384:512], hT[:, k, lt:lt + n], k == 0, k == 7, [wk] + hk, [pqk])
                        act(qs[:, lt:lt + n], pq[:, 0:n], AF.Silu, [pqk], tk("qs", lt, n))
                        pg, pgk = bank()
                        for k in range(8):
                            mm(pg[:, 0:n], w[:, k, 512:640], hT[:, k, lt:lt + n], k == 0, k == 7, [wk] + hk, [pgk])
                        act(gs[:, lt:lt + n], pg[:, 0:n], AF.Silu, [pgk], tk(("gs", hp), lt, n))
                    pools["big"] = [0, 1]

                def T1(tix):
                    lt = tix * 128
                    j = tix % NB
                    pz, pzk = bank()
                    for k in range(8):
                        mm(pz[:, 0:384], hT[:, k, lt:lt + 128], w[:, k, 0:384], k == 0, k == 7,
                           [wk] + tk("hTr", lt, 128), [pzk])
                    act(kt[j], pz[:, 0:256], AF.Exp, [pzk], [("kt", j)])
                    act(vtm[:, tix, :], pz[:, 256:384], AF.Copy, [pzk], [("vtm", hp, tix)])
                    act(kt[j], kt[j], AF.Ln, [("kt", j), "ones_f"], [("kt", j)], bias=ones_f[:, 0:1], scale=1.0)
                    act(kt[j], kt[j], AF.Exp, [("kt", j)], [("kt", j)], scale=-1.0)
                    omlh = oml[:, h, :, :].rearrange("p a d -> p (a d)")
                    if R["masked"]:
                        stt(kt[j], kt[j], maskt[:, tix:tix + 1], omlh, ALU.mult, ALU.mult,
                            [("kt", j), "maskt", "oml"], [("kt", j)])
                    else:
                        tt(kt[j], kt[j], omlh, ALU.mult, [("kt", j), "oml"], [("kt", j)])
                    act(lf[j], kt[j], AF.Ln, [("kt", j), "ones_f"], [("lf", j)], scale=-1.0, bias=ones_f[:, 0:1])

                def T2(tix):
                    lt = tix * 128
                    j = tix % NB
                    jj = tix % 2
                    pG, pGk = qslot4()
                    pB, pBk = qslot4()
                    for dr in range(2):
                        lfd = lf[j][:, dr * 128:(dr + 1) * 128]
                        mm(pG[:, dr * 128:(dr + 1) * 128], lfd, tri[:, dr, :], True, True, [("lf", j), ("tri", dr)], [pGk])
                    for dr in range(2):
                        lfd = lf[j][:, dr * 128:(dr + 1) * 128]
                        mm(pB[:, dr * 128:(dr + 1) * 128], tri[:, 2 + dr, :], lfd, True, True,
                           [("lf", j), ("tri", 2 + dr)], [pBk])
                    egf = eg[jj].rearrange("p a d -> p (a d)")
                    enf = en[jj].rearrange("p a d -> p (a d)")
                    ebf = ebb[jj].rearrange("p a d -> p (a d)")
                    act(egf, pG[:, 0:256], AF.Exp, [pGk], [("eg", jj)])
                    act(enf, pG[:, 0:256], AF.Exp, [pGk], [("en", jj)], scale=-1.0)
                    act(ebf, pB[:, 0:256], AF.Exp, [pBk], [("ebb", jj)])
                    pK, pKk = qslot4()
                    for dr in range(2):
                        ktd = kt[j][:, dr * 128:(dr + 1) * 128]
                        b.op("pe", lambda e, dr=dr, ktd=ktd: e.transpose(pK[:, dr * 128:(dr + 1) * 128], ktd, ident),
                             [("kt", j), "ident"], [pKk])
                    tt(qd[:, :, lt:lt + 128], qs[:, lt:lt + 128].unsqueeze(1).to_broadcast([128, 2, 128]), eg[jj],
                       ALU.mult, tk("qs", lt, 128) + [("eg", jj)], [("qd", hp, 0, tix), ("qd", hp, 1, tix)])
                    tt(ki[jj].rearrange("p a d -> p (a d)"), pK[:, 0:256], enf, ALU.mult, [pKk, ("en", jj)],
                       [("ki", jj)])
                    tt(ksm[:, :, tix, :], kt[j].rearrange("p (a d) -> p a d", a=2), ebb[jj], ALU.mult,
                       [("kt", j), ("ebb", jj)], [("ksm", hp, 0, tix), ("ksm", hp, 1, tix)], e="pool")
                    for dr in range(2):
                        col = CH - 1 if dr == 0 else 0
                        cp(Dc[:, dr, tix * CPT:(tix + 1) * CPT],
                           eg[jj][:, dr, :].rearrange("p (c t) -> p c t", t=CH)[:, :, col],
                           [("eg", jj)], [("Dc", hp, dr, tix)], e="pool")

                def T3(tix):
                    lt = tix * 128
                    jj = tix % 2
                    pS, pSk = qslot4()
                    for dr in range(2):
                        mm(pS[:, dr * 128:(dr + 1) * 128], ki[jj][:, dr, :], qd[:, dr, lt:lt + 128], True, True,
                           [("ki", jj), ("qd", hp, dr, tix)], [pSk])
                    tt(scm[:, :, tix, :], pS[:, 0:256].rearrange("p (a c) -> p a c", a=2), tri[:, 0:2, :], ALU.mult,
                       [pSk, ("tri", 0), ("tri", 1)], [("scm", hp, 0, tix), ("scm", hp, 1, tix)])

                def tstep(i):
                    def f():
                        if i < nt_:
                            T1(i)
                        if 0 <= i - 1 < nt_:
                            T2(i - 1)
                        if 0 <= i - 2 < nt_:
                            T3(i - 2)
                    return f
                Tl = [tstep(i) for i in range(nt_ + 2)]

                def out_stages(tix, fslot):
                    lt = tix * 128
                    jo = ocnt[0] % 2
                    ocnt[0] += 1
                    po, pok = ps[6 + jo][:, 0:128], ("ps", 6 + jo)

                    def O1():
                        seq = [(vtm[:, tix, :], scm[:, dr, tix, :], [("vtm", hp, tix), ("scm", hp, dr, tix)], None)
                               for dr in range(2)]
                        for c4 in range(CPT):
                            gc = tix * CPT + c4
                            fs = fslot + c4
                            seq.append((SprevF[:, fs, :], qd[:, 0, lt + c4 * CH:lt + c4 * CH + CH],
                                        [("SprevF", fs), ("qd", hp, 0, tix)], c4))
                            seq.append((SprevB[:, gc, :], qd[:, 1, lt + c4 * CH:lt + c4 * CH + CH],
                                        [("SprevB", gc), ("qd", hp, 1, tix)], c4))
                        for qi, (lh, rh, rd, c4) in enumerate(seq):
                            outp = po if c4 is None else po[:, c4 * CH:c4 * CH + CH]
                            mm(outp, lh, rh, qi == 0, qi == len(seq) - 1, rd, [pok])
                        act(osq[jo], po, AF.Square, [pok], [("osq", jo)])

                    def O2():
                        pn, pnk = qslot()
                        mm(pn, ones_bf, osq[jo], True, True, [("osq", jo), "ones_bf"], [pnk])
                        act(rso[jo], pn, AF.Ln, [pnk, "epsc"], [("rso", jo)], scale=1.0 / 128, bias=epsc[:, 0:1])
                        act(rso[jo], rso[jo], AF.Exp, [("rso", jo)], [("rso", jo)], scale=-0.5)

                    def O3():
                        tt(o1, po, rso[jo], ALU.mult, [pok, ("rso", jo)], ["o1"])
                        stt(oTh[:, lt:lt + 128], o1, gnw[:, 0:1], gs[:, lt:lt + 128], ALU.mult, ALU.mult,
                            ["o1", "gnw"] + tk(("gs", hp), lt, 128), tk(("oTh", hp), lt, 128))
                    return [O1, O2, O3]

                Sl = []
                seqs = R["seqs"]
                for p0 in range(0, len(seqs), 2):
                    grp = list(enumerate(seqs))[p0:p0 + 2]
                    for dr in (1, 0):
                        ctx = {"state": {}, "pending": [], "step": 0}

                        def init(grp=grp, dr=dr, ctx=ctx):
                            for sl, (si, (s0t, snt)) in enumerate(grp):
                                S = Sst[:, sl * 3, :]
                                Sk = ("Sst", sl, 0)
                                if R["init"]:
                                    cp(S, Sinit[:, dr, h, :], [("Sinit", dr, h)], [Sk], e="pool")
                                else:
                                    b.op("pool", lambda g, S=S: g.memset(S, 0.0), [], [Sk])
                                ctx["state"][sl] = 0
                        Sl.append(init)
                        nch = grp[0][1][1] * CPT
                        order = list(range(nch)) if dr == 0 else list(range(nch - 1, -1, -1))
                        for cn in order:
                            for sl, (si, (s0t, snt)) in enumerate(grp):
                                def stepf(cn=cn, sl=sl, s0t=s0t, dr=dr, ctx=ctx):
                                    ctx["step"] += 1
                                    step = ctx["step"]
                                    pp = ctx["state"][sl]
                                    S = Sst[:, sl * 3 + pp, :]
                                    Sk = ("Sst", sl, pp)
                                    S2 = Sst[:, sl * 3 + (pp + 1) % 3, :]
                                    S2k = ("Sst", sl, (pp + 1) % 3)
                                    gc = s0t * CPT + cn
                                    tix = gc // CPT
                                    r0 = (gc % CPT) * CH
                                    if dr == 1:
                                        dstS, dstk = SprevB[:, gc, :], ("SprevB", gc)
                                    else:
                                        fs = (sl * RING + cn % RING) % NSF
                                        dstS, dstk = SprevF[:, fs, :], ("SprevF", fs)
                                    if step % 3 == 0:
                                        cp(dstS, S, [Sk], [dstk], e="pool")
                                    else:
                                        act(dstS, S, AF.Copy, [Sk], [dstk])
                                    pkv, pkvk = qslot()
                                    mm(pkv, ksm[r0:r0 + CH, dr, tix, :], vtm[r0:r0 + CH, tix, :], True, True,
                                       [("ksm", hp, dr, tix), ("vtm", hp, tix)], [pkvk], tile_position=(r0, 0))
                                    stt(S2, S, Dc[:, dr, gc:gc + 1], pkv, ALU.mult, ALU.add,
                                        [pkvk, ("Dc", hp, dr, tix), Sk], [S2k])
                                    ctx["state"][sl] = (pp + 1) % 3
                                    if dr == 0 and gc % CPT == CPT - 1:
                                        st3 = out_stages(tix, (sl * RING + (cn - (CPT - 1)) % RING) % NSF)
                                        ctx["pending"].append((step + 1, st3[0]))
                                        ctx["pending"].append((step + 2, st3[1]))
                                        ctx["pending"].append((step + 3, st3[2]))
                                    due = [f for (t_, f) in ctx["pending"] if t_ <= step]
                                    ctx["pending"] = [(t_, f) for (t_, f) in ctx["pending"] if t_ > step]
                                    for f in due:
                                        f()
                                Sl.append(stepf)

                        def fin(grp=grp, dr=dr, ctx=ctx):
                            for (t_, f) in ctx["pending"]:
                                f()
                            ctx["pending"] = []
                            if R["out_state"]:
                                for sl, (si, (s0t, snt)) in enumerate(grp):
                                    pp = ctx["state"][sl]
                                    b.dma("sp", "nso", ns_d[si, dr, h], Sst[:, sl * 3 + pp, :],
                                          reads=[("Sst", sl, pp)], writes=[("nsd", si, dr, h)])
                        Sl.append(fin)

                def wout():
                    for m in range(8):
                        for (t0, n) in rtiles:
                            lt = t0 - T0
                            v = 0 if t0 < NPR else 1
                            pst, pk = bank()
                            mm(pst[:, 0:n], wo[:, m * 128:(m + 1) * 128], oTh[:, lt:lt + n], True, True,
                               [("wo", hp)] + tk(("oTh", hp), lt, n), [pk])
                            stt(xT[:, m, t0:t0 + n], pst[:, 0:n], modv[:, l, 16 + m, v:v + 1], xT[:, m, t0:t0 + n],
                                ALU.mult, ALU.add, [pk, "modv"] + tk("xT", t0, n), tk("xT", t0, n))
                Sl.append(wout)
                return A, Tl, Sl

            prevS = []
            for h in range(8):
                A, Tl, Sl = make_head(h)
                A()
                nT = len(Tl)
                nS = len(prevS)
                si_ = 0
                for ti_, tf in enumerate(Tl):
                    tf()
                    upto_ = (nS * (ti_ + 1)) // nT
                    while si_ < upto_:
                        prevS[si_]()
                        si_ += 1
                prevS = Sl
            for f in prevS:
                f()
            pools["big"] = [0, 1, 2, 3, 4, 5, 6, 7]
            pools["small"] = [4, 5, 6, 7]

    def dump(i):
        if debug:
            for k in range(8):
                b.dma("sp", "dbg", dbg_d[i, k * 128:(k + 1) * 128, :], xT[:, k, :], reads=tk("xT", 0, NT),
                      writes=[("dbg", i, k)])

    if upto >= 2:
        hgrn_summary()
    if upto >= 3:
        hgrn_main(0)
    dump(0)
    if upto >= 4:
        conv_ffn(0, TILES0)
    dump(1)
    if upto >= 5:
        sconv(1)
    dump(2)
    if upto >= 6:
        conv_ffn(1, TILES1)
    dump(3)
    b.barrier()
    cvf = Carve()
    fo = arena[:, 0:8 * 1024].rearrange("p (k n) -> p k n", k=8)
    cvf.off = 8 * 1024 * 4
    for half, (tiles, dd, base) in enumerate([(TILES1[0:2], yT_p_d, 0), (TILES1[2:4], yT_s_d, OWN0)]):
        cvh = Carve()
        cvh.off = cvf.off

        class V:
            pass
        sq = cvh.take([8, 512], BF16)
        rs = cvh.take([1, 512], F32)[:, 0, :]
        for (t0, n) in tiles:
            rk = tk("xT", t0, n)
            act(sq[:, :, 0:n], xT[:, :, t0:t0 + n], AF.Square, rk, ["nm_sq"])
            pst, pk = bank()
            for k in range(8):
                mm(pst[:, 0:n], ones_bf, sq[:, k, 0:n], k == 0, k == 7, ["nm_sq", "ones_bf"], [pk])
            act(rs[:, 0:n], pst[:, 0:n], AF.Ln, [pk, "epsc"], ["nm_rs"], scale=1.0 / D, bias=epsc[:, 0:1])
            act(rs[:, 0:n], rs[:, 0:n], AF.Exp, ["nm_rs"], ["nm_rs"], scale=-0.5)
            for k in range(8):
                stt(fo[:, k, t0 - base:t0 - base + n], xT[:, k, t0:t0 + n], nfin[:, k:k + 1], rs[:, 0:n], ALU.mult,
                    ALU.mult, rk + ["nm_rs", "nfin"], [("fo", k)])
        for k in range(8):
            b.dma("sp", "yout", dd[k * 128:(k + 1) * 128, :], fo[:, k, :], reads=[("fo", k)], writes=[("yo", half, k)])
    b.wait_all("sp")
    return nc, es


_CACHE = {}


def _prep_inputs(inp):
    f = np.float32
    x_prompt = np.asarray(inp["x_prompt"], f)
    x_sample = np.asarray(inp["x_sample"], f)
    state = np.asarray(inp["state_hgrn"], f)
    c = np.asarray(inp["c"], f)
    c_ctx = np.asarray(inp["c_ctx"], f)

    def fm(vec, nchunk):
        v = vec.reshape(vec.shape[:-1] + (nchunk, 128))
        return np.ascontiguousarray(np.moveaxis(v, -1, 0))

    shared = {}
    shared["ada_w"] = np.ascontiguousarray(inp["ada_w"], f)
    shared["ada_b"] = fm(np.asarray(inp["ada_b"], f), 48)
    shared["nmix"] = fm(np.asarray(inp["norm_mix_w"], f), 8)
    shared["nffn"] = fm(np.asarray(inp["norm_ffn_w"], f), 8)
    shared["nfin"] = fm(np.asarray(inp["final_norm_w"], f), 8)
    w_in = np.asarray(inp["hgrn_w_in"], f)[0]
    hw = np.empty((8, D, 640), f)
    for h in range(8):
        for q, base in enumerate([1024, 2048, 3072, 0, 4096]):
            hw[h, :, q * 128:(q + 1) * 128] = w_in[:, base + h * 128:base + (h + 1) * 128]
    shared["hw_in"] = hw
    shared["lbraw"] = np.ascontiguousarray(inp["hgrn_lower_bounds"], f)
    shared["gnw"] = np.ascontiguousarray(np.asarray(inp["hgrn_gnorm_w"], f)[0].reshape(128, 1))
    shared["hw_out"] = np.ascontiguousarray(np.asarray(inp["hgrn_w_out"], f)[0])
    shared["sw_in"] = np.ascontiguousarray(np.asarray(inp["sconv_w_in"], f)[0])
    shared["scw"] = np.ascontiguousarray(np.moveaxis(np.asarray(inp["sconv_conv_w"], f)[0].reshape(3, 8, 128), -1, 0))
    shared["scb"] = fm(np.asarray(inp["sconv_conv_b"], f)[0], 8)
    shared["sw_out"] = np.ascontiguousarray(np.asarray(inp["sconv_w_out"], f)[0])
    shared["fw_up"] = np.ascontiguousarray(inp["ffn_w_up"], f)
    shared["fcw"] = np.ascontiguousarray(np.moveaxis(np.asarray(inp["ffn_conv_w"], f).reshape(2, 3, 22, 128), -1, 0))
    shared["fcb"] = fm(np.asarray(inp["ffn_conv_b"], f), 22)
    shared["fw_dn"] = np.ascontiguousarray(inp["ffn_w_down"], f)

    maps = []
    for core in range(8):
        bi, j = core // 4, core % 4
        m = dict(shared)
        xp = x_prompt[4 * core:4 * core + 4].reshape(NPR, D)
        m["xT_p"] = np.ascontiguousarray(xp.T)
        lo, hi = 1024 * j - 64, 1024 * (j + 1) + 64
        xs = np.zeros((NSX, D), f)
        mask_s = np.zeros(NSX, f)
        a, e = max(lo, 0), min(hi, 4096)
        xs[a - lo:e - lo] = x_sample[bi, a:e]
        mask_s[a - lo:e - lo] = 1.0
        m["xT_s"] = np.ascontiguousarray(xs.T)
        pre = np.zeros((NREG, D), f)
        mask_pre = np.zeros(NREG, f)
        npre = max(lo, 0)
        if npre > 0:
            pre[NREG - npre:] = x_sample[bi, 0:npre]
            mask_pre[NREG - npre:] = 1.0
        m["xT_pre"] = np.ascontiguousarray(pre.T)
        suf = np.zeros((NREG, D), f)
        mask_suf = np.zeros(NREG, f)
        nsuf = max(4096 - hi, 0)
        if nsuf > 0:
            suf[0:nsuf] = x_sample[bi, hi:4096]
            mask_suf[0:nsuf] = 1.0
        m["xT_suf"] = np.ascontiguousarray(suf.T)
        mk_ = np.concatenate([mask_s.reshape(9, 128), mask_pre.reshape(24, 128), mask_suf.reshape(24, 128)], 0)
        m["mask_tm"] = np.ascontiguousarray(mk_.T)
        hm = np.zeros((128, 2), f)
        hm[:, 0] = 1.0 if lo >= 0 else 0.0
        hm[:, 1] = 1.0 if hi <= 4096 else 0.0
        m["hmask"] = hm
        m["s0"] = np.ascontiguousarray(np.transpose(state[bi, 0], (2, 0, 1, 3)))
        cc = np.stack([c_ctx, c[bi]], -1).reshape(8, 128, 2)
        m["cT"] = np.ascontiguousarray(np.transpose(cc, (1, 0, 2)))
        maps.append(m)
    return maps


def kernel(**inputs):
    debug = bool(inputs.pop("_debug", False))
    upto = int(inputs.pop("_upto", 99))
    key = ("prog", debug, upto)
    if key not in _CACHE:
        _CACHE[key] = build_program(debug, upto)
    nc, _es = _CACHE[key]
    maps = _prep_inputs(inputs)
    res = run_bass_kernel_spmd(nc, maps, core_ids=list(range(8)))
    r = res.results
    y_prompt = np.empty((32, 256, D), np.float32)
    y_sample = np.empty((2, 4096, D), np.float32)
    new_state = np.empty((32, 1, 2, 8, 128, 128), np.float32)
    for core in range(8):
        bi, j = core // 4, core % 4
        y_prompt[4 * core:4 * core + 4] = r[core]["yT_p"].T.reshape(4, 256, D)
        y_sample[bi, 1024 * j:1024 * (j + 1)] = r[core]["yT_s"].T
        new_state[4 * core:4 * core + 4, 0] = r[core]["ns"]
    if debug:
        return (y_prompt, y_sample, new_state), [r[c]["dbg"] for c in range(8)]
    return (y_prompt, y_sample, new_state)
```

```python
from contextlib import ExitStack
import numpy as np
import concourse.bass as bass
import concourse.mybir as mybir
from concourse.bass_utils import run_bass_kernel_spmd

F32 = mybir.dt.float32
BF16 = mybir.dt.bfloat16
AF = mybir.ActivationFunctionType
ALU = mybir.AluOpType

D = 1024
NPR = 1024
NSX = 1152
NT = NPR + NSX
OWN0 = NPR + 64
NREG = 3072
DFF = 2816
EPS = 1e-6
TILES0 = [(0, 512), (512, 512), (1024, 384), (1408, 384), (1792, 384)]
TILES1 = [(0, 512), (512, 512), (OWN0, 512), (OWN0 + 512, 512)]
FFN_GROUPS = [(0, 8), (8, 7), (15, 7)]
CH = 64
CPT = 128 // CH


class TL:
    def __init__(self, sem, step):
        self.sem = sem
        self.count = 0
        self.step = step


class Builder:
    def __init__(self, nc, es):
        self.nc = nc
        self.es = es
        self.eng = {"pe": nc.tensor, "act": nc.scalar, "dve": nc.vector, "pool": nc.gpsimd, "sp": nc.sync}
        self.tl = {}
        for e in self.eng:
            self.tl[e] = TL(es.enter_context(nc.semaphore("s_" + e)), 1)
        self.seen = {e: {} for e in self.eng}
        self.res = {}

    def timeline(self, name):
        if name not in self.tl:
            self.tl[name] = TL(self.es.enter_context(self.nc.semaphore("d_" + name)), 16)
        return self.tl[name]

    def _waits(self, e, reads, writes):
        need = {}
        for r in reads:
            st = self.res.get(r)
            if st and st["w"]:
                t, v = st["w"]
                need[t] = max(need.get(t, 0), v)
        for w in writes:
            st = self.res.get(w)
            if st:
                if st["w"]:
                    t, v = st["w"]
                    need[t] = max(need.get(t, 0), v)
                for t, v in st["r"].items():
                    need[t] = max(need.get(t, 0), v)
        for t, v in need.items():
            if e == "pe" and t == "pe":
                continue
            if self.tl[t].step == 16:
                v = self.tl[t].count
            if self.seen[e].get(t, 0) < v:
                self.eng[e].wait_ge(self.tl[t].sem, v)
                self.seen[e][t] = v

    def _record(self, t, v, reads, writes):
        for r in reads:
            self.res.setdefault(r, {"w": None, "r": {}})["r"][t] = v
        for w in writes:
            self.res[w] = {"w": (t, v), "r": {}}

    def op(self, e, fn, reads=(), writes=()):
        self._waits(e, reads, writes)
        ins = fn(self.eng[e])
        t = self.tl[e]
        t.count += 1
        ins.then_inc(t.sem, 1)
        self._record(e, t.count, reads, writes)

    def dma(self, e, tlname, out, in_, reads=(), writes=(), **kw):
        self._waits(e, reads, writes)
        t = self.timeline(tlname)
        ins = self.eng[e].dma_start(out=out, in_=in_, **kw)
        t.count += 16
        ins.then_inc(t.sem, 16)
        self._record(tlname, t.count, reads, writes)

    def barrier(self):
        for e in self.eng:
            self.wait_all(e)

    def wait_all(self, e):
        for name, t in self.tl.items():
            if t.count > 0 and self.seen[e].get(name, 0) < t.count:
                self.eng[e].wait_ge(t.sem, t.count)
                self.seen[e][name] = t.count


def tk(name, t0, n):
    return [(name, b) for b in range(t0 // 64, (t0 + n + 63) // 64)]


def build_program(debug=False, upto=99):
    nc = bass.Bass("TRN2", target_bir_lowering=False, dynamic_dma_scratch_size=8192)
    es = ExitStack()
    b = Builder(nc, es)

    def din(name, shape):
        return nc.dram_tensor(name, shape, F32, kind="ExternalInput").ap()

    def dout(name, shape):
        return nc.dram_tensor(name, shape, F32, kind="ExternalOutput").ap()

    xT_p_d = din("xT_p", [D, NPR])
    xT_s_d = din("xT_s", [D, NSX])
    xT_pre_d = din("xT_pre", [D, NREG])
    xT_suf_d = din("xT_suf", [D, NREG])
    mask_d = din("mask_tm", [128, 9 + 24 + 24])
    hmask_d = din("hmask", [128, 2])
    s0_d = din("s0", [128, 2, 8, 128])
    cT_d = din("cT", [128, 8, 2])
    ada_w_d = din("ada_w", [2, D, 6 * D])
    ada_b_d = din("ada_b", [128, 2, 48])
    nmix_d = din("nmix", [128, 2, 8])
    nffn_d = din("nffn", [128, 2, 8])
    nfin_d = din("nfin", [128, 8])
    hw_in_d = din("hw_in", [8, D, 640])
    lbraw_d = din("lbraw", [2, 2, D])
    gnw_d = din("gnw", [128, 1])
    hw_out_d = din("hw_out", [D, D])
    sw_in_d = din("sw_in", [D, 3 * D])
    scw_d = din("scw", [128, 3, 8])
    scb_d = din("scb", [128, 8])
    sw_out_d = din("sw_out", [D, D])
    fw_up_d = din("fw_up", [2, D, 2 * DFF])
    fcw_d = din("fcw", [128, 2, 3, 22])
    fcb_d = din("fcb", [128, 2, 22])
    fw_dn_d = din("fw_dn", [2, DFF, D])
    yT_p_d = dout("yT_p", [D, NPR])
    yT_s_d = dout("yT_s", [D, NPR])
    ns_d = dout("ns", [4, 2, 8, 128, 128])
    dbg_d = dout("dbg", [4, D, NT]) if debug else None

    def sb(name, shape, dt):
        return es.enter_context(nc.sbuf_tensor(name, shape, dt))

    xT = sb("xT", [128, 8, NT], F32)[:]
    actB = sb("actB", [128, 8 * NT], BF16)[:]
    oT = actB.rearrange("p (k n) -> p k n", k=8)
    xstage = actB.bitcast(F32)[:, 0:8 * 1024].rearrange("p (k n) -> p k n", k=8)
    ARENA_F32 = 17408
    arena = sb("arena", [128, ARENA_F32], F32)[:]
    arena_bf = arena.bitcast(BF16)

    class Carve:
        def __init__(self, base_f32=None, base_bf=None, limit=None):
            self.off = 0
            self.bf32 = arena if base_f32 is None else base_f32
            self.bbf = arena_bf if base_bf is None else base_bf
            self.limit = ARENA_F32 * 4 if limit is None else limit

        def take(self, shape, dt):
            n = int(np.prod(shape))
            esz = 4 if dt == F32 else 2
            self.off = (self.off + 31) // 32 * 32
            o = self.off
            self.off += n * esz
            assert self.off <= self.limit, ("arena overflow", self.off)
            base = self.bf32 if dt == F32 else self.bbf
            v = base[:, o // esz:o // esz + n]
            if len(shape) == 2:
                return v.rearrange("p (a b) -> p a b", a=shape[0])
            if len(shape) == 3:
                return v.rearrange("p (a b c) -> p a b c", a=shape[0], b=shape[1])
            return v

    NW = 2
    wslot = [sb("wslot%d" % i, [128, 8, 640], BF16)[:] for i in range(NW)]
    ps = [es.enter_context(nc.psum_tensor("ps%d" % i, [128, 512], F32))[:] for i in range(8)]

    ones_bf = sb("ones_bf", [128, 128], BF16)[:]
    ones_f = sb("ones_f", [128, 128], F32)[:]
    ident = sb("ident", [128, 128], F32)[:]
    tri = sb("tri", [128, 6, 128], F32)[:]
    iop = sb("iop", [128, 1], F32)[:]
    modv = sb("modv", [128, 2, 48, 2], F32)[:]
    adab = sb("adab", [128, 2, 48], F32)[:]
    nmix = sb("nmix_s", [128, 2, 8], F32)[:]
    nffn = sb("nffn_s", [128, 2, 8], F32)[:]
    nfin = sb("nfin_s", [128, 8], F32)[:]
    Amod = sb("Amod", [128, 5, 8, 2], F32)[:]
    cT = sb("cT_s", [128, 8, 2], F32)[:]
    scT = sb("scT", [128, 8, 2], BF16)[:]
    oml = sb("oml", [128, 8, 2, 128], F32)[:]
    gnw = sb("gnw_s", [128, 1], F32)[:]
    scw = sb("scw_s", [128, 3, 8], F32)[:]
    scb = sb("scb_s", [128, 8], F32)[:]
    fcw = sb("fcw_s", [128, 2, 3, 22], F32)[:]
    fcb = sb("fcb_s", [128, 2, 22], F32)[:]
    maskt = sb("maskt", [128, 57], F32)[:]
    hmask = sb("hmask_s", [128, 2], F32)[:]
    Sinit = sb("Sinit", [128, 2, 8, 128], F32)[:]
    epsc = sb("epsc", [128, 1], F32)[:]

    cnt = {"w": 0, "ps": 0, "q": 0}

    pools = {"big": [0, 1, 2, 3, 4, 5, 6, 7], "small": [4, 5, 6, 7]}

    def bank():
        i = pools["big"][cnt["ps"] % len(pools["big"])]
        cnt["ps"] += 1
        return ps[i], ("ps", i)

    def qslot():
        i = pools["small"][cnt["q"] % len(pools["small"])]
        cnt["q"] += 1
        return ps[i][:, 0:128], ("ps", i)

    def qslot4():
        i = pools["small"][cnt["q"] % len(pools["small"])]
        cnt["q"] += 1
        return ps[i], ("ps", i)

    def wload(src3, ncols, reads=()):
        i = cnt["w"] % NW
        cnt["w"] += 1
        K = src3.shape[1]
        dst = wslot[i].rearrange("p k n -> p (k n)")[:, 0:K * ncols].rearrange("p (k n) -> p k n", k=K)
        b.dma("pool", "w%d" % i, dst, src3, reads=list(reads), writes=[("w", i)])
        return dst, ("w", i)

    def wview(w2d, c0, ncols):
        return w2d.rearrange("(k p) n -> p k n", p=128)[:, :, c0:c0 + ncols]

    def mm(out, lhsT, rhs, start, stop, reads, writes, **kw):
        b.op("pe", lambda e: e.matmul(out, lhsT=lhsT, rhs=rhs, start=start, stop=stop, **kw), reads, writes)

    def act(out, in_, func, reads, writes, **kw):
        b.op("act", lambda e: e.activation(out=out, in_=in_, func=func, **kw), reads, writes)

    def tt(out, in0, in1, op, reads, writes, e="dve"):
        b.op(e, lambda g: g.tensor_tensor(out=out, in0=in0, in1=in1, op=op), reads, writes)

    def ts(out, in0, s1, s2, op0, op1, reads, writes, e="dve"):
        if s2 is None:
            b.op(e, lambda g: g.tensor_scalar(out=out, in0=in0, scalar1=s1, scalar2=None, op0=op0), reads, writes)
        else:
            b.op(e, lambda g: g.tensor_scalar(out=out, in0=in0, scalar1=s1, scalar2=s2, op0=op0, op1=op1), reads, writes)

    def stt(out, in0, scalar, in1, op0, op1, reads, writes):
        b.op("dve", lambda g: g.scalar_tensor_tensor(out=out, in0=in0, scalar=scalar, in1=in1, op0=op0, op1=op1),
             reads, writes)

    def cp(out, in_, reads, writes, e="dve"):
        b.op(e, lambda g: g.tensor_copy(out=out, in_=in_), reads, writes)

    def ld(dst, src, key):
        b.dma("sp", "c_" + str(key), dst, src, writes=[key])

    cv0 = Carve()
    lbt = cv0.take([2, 2, D], F32)
    iot = cv0.take([1, 128], F32)[:, 0, :]
    tmpc = cv0.take([4, 128], F32)
    pc = cv0.take([1, 2], F32)[:, 0, :]
    sinit_keys = [("Sinit", dr, h) for dr in range(2) for h in range(8)]
    for dst, src, key in [(adab, ada_b_d, ["adab"]), (nmix, nmix_d, ["nmix"]), (nffn, nffn_d, ["nffn"]),
                          (nfin, nfin_d, ["nfin"]), (cT, cT_d, ["cT"]), (gnw, gnw_d, ["gnw"]), (scw, scw_d, ["scw"]),
                          (scb, scb_d, ["scb"]), (fcw, fcw_d, ["fcw"]), (fcb, fcb_d, ["fcb"]),
                          (maskt, mask_d, ["maskt"]), (hmask, hmask_d, ["hmask"]), (Sinit, s0_d, sinit_keys)]:
        b.dma("sp", "cst", dst, src, writes=key)
    b.dma("sp", "cst", lbt.rearrange("p a b d -> p (a b d)"),
          lbraw_d.rearrange("a b d -> (a b d)").partition_broadcast(128), writes=["lbt"])

    b.op("pool", lambda g: g.memset(ones_bf, 1.0), [], ["ones_bf"])
    b.op("pool", lambda g: g.memset(ones_f, 1.0), [], ["ones_f"])
    b.op("pool", lambda g: g.memset(epsc, EPS), [], ["epsc"])
    b.op("pool", lambda g: g.iota(iot, pattern=[[1, 128]], base=0, channel_multiplier=0,
                                  allow_small_or_imprecise_dtypes=True), [], ["iot"])
    b.op("pool", lambda g: g.iota(iop, pattern=[[0, 1]], base=0, channel_multiplier=1,
                                  allow_small_or_imprecise_dtypes=True), [], ["iop"])
    b.op("pool", lambda g: g.iota(tmpc[:, 1, :], pattern=[[1, 128 // CH], [0, CH]], base=0, channel_multiplier=0,
                                  allow_small_or_imprecise_dtypes=True), [], ["tc1"])
    ts(pc[:, 0:1], iop, float(CH), None, ALU.is_ge, None, ["iop"], ["pc0"])
    for thr in range(2 * CH, 128, CH):
        ts(pc[:, 1:2], iop, float(thr), None, ALU.is_ge, None, ["iop", "pc0"], ["pc1"])
        tt(pc[:, 0:1], pc[:, 0:1], pc[:, 1:2], ALU.add, ["pc0", "pc1"], ["pc0"])
    ts(tmpc[:, 0, :], iot, iop[:, 0:1], None, ALU.subtract, None, ["iot", "iop"], ["tc0"])
    ts(tmpc[:, 3, :], tmpc[:, 1, :], pc[:, 0:1], None, ALU.is_equal, None, ["tc1", "pc0"], ["tc3"])
    dmp = tmpc[:, 0, :]
    same = tmpc[:, 3, :]
    ts(ident, dmp, 0.0, None, ALU.is_equal, None, ["tc0"], ["ident"])
    for i, (opc, thr) in enumerate([(ALU.is_ge, 0.0), (ALU.is_le, 0.0), (ALU.is_lt, 0.0), (ALU.is_gt, 0.0)]):
        ts(tri[:, i, :], dmp, thr, None, opc, None, ["tc0"], [("tri", i)])
        tt(tri[:, i, :], tri[:, i, :], same, ALU.mult, [("tri", i), "tc3"], [("tri", i)])
    ts(tri[:, 4, :], dmp, 0.0, None, ALU.is_lt, None, ["tc0"], [("tri", 4)])
    ts(tri[:, 5, :], dmp, 0.0, None, ALU.is_gt, None, ["tc0"], [("tri", 5)])

    tt(lbt[:, :, 0, :], lbt[:, :, 1, :], lbt[:, :, 0, :], ALU.subtract, ["lbt"], ["lbt"])
    for dr in range(2):
        act(oml[:, :, dr, :], lbt[:, dr, 0, :].rearrange("p (h d) -> p h d", h=8), AF.Sigmoid, ["lbt"], ["oml"])

    act(scT.rearrange("p k v -> p (k v)"), cT.rearrange("p k v -> p (k v)"), AF.Silu, ["cT"], ["scT"])
    def ada_group(l, g, slot=None):
        if slot is None:
            w, wk = wload(wview(ada_w_d[l], g * 512, 512), 512)
        else:
            w, wk = slot, "wada"
            b.dma("pool", "wada", slot, wview(ada_w_d[l], g * 512, 512), writes=["wada"])
        pst, pk = bank()
        for m4 in range(4):
            for k in range(8):
                mm(pst[:, m4 * 2:m4 * 2 + 2], w[:, k, m4 * 128:(m4 + 1) * 128], scT[:, k, :], k == 0, k == 7,
                   [wk, "scT"], [pk])
        for v in range(2):
            tt(modv[:, l, g * 4:(g + 1) * 4, v], pst[:, 0:8].rearrange("p (m v) -> p m v", v=2)[:, :, v],
               adab[:, l, g * 4:(g + 1) * 4], ALU.add, [pk, "adab"], ["modv"])

    def ada_finish(l):
        for i in (1, 4):
            ts(modv[:, l, i * 8:(i + 1) * 8, :], modv[:, l, i * 8:(i + 1) * 8, :], 1.0, None, ALU.add, None,
               ["modv"], ["modv"])
        for v in range(2):
            tt(Amod[:, 2 * l, :, v], nmix[:, l, :], modv[:, l, 8:16, v], ALU.mult, ["modv", "nmix"], ["Amod"])
            tt(Amod[:, 2 * l + 1, :, v], nffn[:, l, :], modv[:, l, 32:40, v], ALU.mult, ["modv", "nffn"], ["Amod"])

    for g in range(12):
        ada_group(0, g)
    ada_finish(0)
    ada_pending = [(1, g) for g in range(12)]
    ada_done = []
    for v in range(2):
        cp(Amod[:, 4, :, v], nfin, ["nfin"], ["Amod"])

    for k in range(8):
        b.dma("sp", "xin", xT[:, k, 0:NPR], xT_p_d[k * 128:(k + 1) * 128, :], writes=tk("xT", 0, NPR))
        b.dma("sp", "xin", xT[:, k, NPR:NT], xT_s_d[k * 128:(k + 1) * 128, :], writes=tk("xT", NPR, NSX))

    def norm_mod(cv, tiles, ai, shift_i, l, src, srck, dst, dstk, vsel=None, final_out=None):
        sq = cv.take([8, 512], BF16)
        rs = cv.take([1, 512], F32)[:, 0, :]
        tmp = [cv.take([1, 512], F32)[:, 0, :] for _ in range(2)]
        for ti, (t0, n) in enumerate(tiles):
            v = (0 if t0 < NPR else 1) if vsel is None else vsel
            rk = tk(srck, t0, n)
            act(sq[:, :, 0:n], src[:, :, t0:t0 + n], AF.Square, rk, ["nm_sq"])
            pst, pk = bank()
            for k in range(8):
                mm(pst[:, 0:n], ones_bf, sq[:, k, 0:n], k == 0, k == 7, ["nm_sq", "ones_bf"], [pk])
            act(rs[:, 0:n], pst[:, 0:n], AF.Ln, [pk, "epsc"], ["nm_rs"], scale=1.0 / D, bias=epsc[:, 0:1])
            act(rs[:, 0:n], rs[:, 0:n], AF.Exp, ["nm_rs"], ["nm_rs"], scale=-0.5)
            for k in range(8):
                if final_out is not None:
                    stt(final_out[:, k, t0:t0 + n], src[:, k, t0:t0 + n], Amod[:, ai, k, v:v + 1], rs[:, 0:n],
                        ALU.mult, ALU.mult, rk + ["nm_rs", "Amod"], tk(dstk, t0, n))
                    continue
                tb = tmp[k % 2]
                stt(tb[:, 0:n], src[:, k, t0:t0 + n], Amod[:, ai, k, v:v + 1], rs[:, 0:n], ALU.mult, ALU.mult,
                    rk + ["nm_rs", "Amod"], [("nm_t", k % 2)])
                act(dst[:, k, t0:t0 + n], tb[:, 0:n], AF.Identity, [("nm_t", k % 2), "modv"], tk(dstk, t0, n),
                    bias=modv[:, l, shift_i * 8 + k, v:v + 1], scale=1.0)

    def out_proj(w2d, src, srck, tiles, l, gate_i, nk=8):
        for g in range(2):
            w, wk = wload(wview(w2d, g * 512, 512), 512)
            for m4 in range(4):
                m = g * 4 + m4
                for (t0, n) in tiles:
                    v = 0 if t0 < NPR else 1
                    pst, pk = bank()
                    for k in range(nk):
                        mm(pst[:, 0:n], w[:, k, m4 * 128:(m4 + 1) * 128], src[:, k, t0:t0 + n], k == 0, k == nk - 1,
                           [wk] + tk(srck, t0, n), [pk])
                    stt(xT[:, m, t0:t0 + n], pst[:, 0:n], modv[:, l, gate_i * 8 + m, v:v + 1], xT[:, m, t0:t0 + n],
                        ALU.mult, ALU.add, [pk, "modv"] + tk("xT", t0, n), tk("xT", t0, n))

    def conv_ffn(l, tiles):
        b.barrier()
        cv = Carve()
        hT = cv.take([8, NT], BF16)
        norm_mod(cv, tiles, 2 * l + 1, 3, l, xT, "xT", hT, "hT")
        t1 = [cv.take([1, 512], F32)[:, 0, :] for _ in range(2)]
        s1 = [cv.take([1, 512], F32)[:, 0, :] for _ in range(2)]
        ada_slot = cv.take([8, 512], BF16) if ada_pending else None
        mT = oT
        wup = fw_up_d[l]
        for (c0, G) in FFN_GROUPS:
            for ci in range(G):
                c = c0 + ci
                i = cnt["w"] % NW
                cnt["w"] += 1
                w = wslot[i][:, :, 0:256]
                b.dma("pool", "w%d" % i, w[:, :, 0:128], wview(wup, c * 128, 128), writes=[("w", i)])
                b.dma("pool", "w%d" % i, w[:, :, 128:256], wview(wup, DFF + c * 128, 128), writes=[("w", i)])
                wk = ("w", i)
                for ti, (t0, n) in enumerate(tiles):
                    seg = 256 if t0 < NPR else 64
                    nsg = n // seg
                    pa, pak = bank()
                    pg, pgk = bank()
                    hk = tk("hT", t0, n)
                    for k in range(8):
                        mm(pa[:, 0:n], w[:, k, 0:128], hT[:, k, t0:t0 + n], k == 0, k == 7, [wk] + hk, [pak])
                    for k in range(8):
                        mm(pg[:, 0:n], w[:, k, 128:256], hT[:, k, t0:t0 + n], k == 0, k == 7, [wk] + hk, [pgk])
                    j = ti % 2
                    a1 = t1[j]
                    act(a1[:, 0:n], pa[:, 0:n], AF.Identity, [pak, "fcw", "fcb"], [("ff_t", j)],
                        scale=fcw[:, l, 1, c:c + 1], bias=fcb[:, l, c:c + 1])
                    a3 = a1[:, 0:n].rearrange("p (s t) -> p s t", t=seg)
                    p3 = pa[:, 0:n].rearrange("p (s t) -> p s t", t=seg)
                    stt(a3[:, :, 1:seg], p3[:, :, 0:seg - 1], fcw[:, l, 0, c:c + 1], a3[:, :, 1:seg], ALU.mult, ALU.add,
                        [pak, ("ff_t", j), "fcw"], [("ff_t", j)])
                    stt(a3[:, :, 0:seg - 1], p3[:, :, 1:seg], fcw[:, l, 2, c:c + 1], a3[:, :, 0:seg - 1], ALU.mult,
                        ALU.add, [pak, ("ff_t", j), "fcw"], [("ff_t", j)])
                    act(s1[j][:, 0:n], a1[:, 0:n], AF.Silu, [("ff_t", j)], [("ff_s", j)])
                    tt(mT[:, ci, t0:t0 + n], pg[:, 0:n], s1[j][:, 0:n], ALU.mult, [pgk, ("ff_s", j)],
                       tk(("mT", ci), t0, n))
                if ada_pending:
                    ada_group(*ada_pending.pop(0), slot=ada_slot)
            if ada_pending and c0 + G >= 22:
                while ada_pending:
                    ada_group(*ada_pending.pop(0), slot=ada_slot)
            if l == 0 and c0 + G >= 22 and not ada_done:
                ada_finish(1)
                ada_done.append(1)
            wdn = fw_dn_d[l]
            for g in range(2):
                wsrc = wdn.rearrange("(k p) n -> p k n", p=128)[:, c0:c0 + G, g * 512:(g + 1) * 512]
                w, wk = wload(wsrc, 512)
                for m4 in range(4):
                    m = g * 4 + m4
                    for (t0, n) in tiles:
                        v = 0 if t0 < NPR else 1
                        pst, pk = bank()
                        for ci in range(G):
                            mm(pst[:, 0:n], w[:, ci, m4 * 128:(m4 + 1) * 128], mT[:, ci, t0:t0 + n], ci == 0,
                               ci == G - 1, [wk] + tk(("mT", ci), t0, n), [pk])
                        stt(xT[:, m, t0:t0 + n], pst[:, 0:n], modv[:, l, 40 + m, v:v + 1], xT[:, m, t0:t0 + n],
                            ALU.mult, ALU.add, [pk, "modv"] + tk("xT", t0, n), tk("xT", t0, n))

    def sconv(l):
        b.barrier()
        cv = Carve()
        hT = cv.take([8, NT], BF16)
        cvn = Carve()
        cvn.off = cv.off
        norm_mod(cvn, TILES0, 2 * l, 0, l, xT, "xT", hT, "hT")
        b.barrier()
        cu = cv.take([1, NT], F32)[:, 0, :]
        gb = cv.take([1, NT], BF16)[:, 0, :]
        ub1 = cv.take([1, 512], F32)[:, 0, :]
        ub = [ub1, ub1]
        zb = cv.take([1, NT], F32)[:, 0, :]
        pT = oT
        for c in range(8):
            i = cnt["w"] % NW
            cnt["w"] += 1
            w = wslot[i][:, :, 0:384]
            for q in range(3):
                b.dma("pool", "w%d" % i, w[:, :, q * 128:(q + 1) * 128], wview(sw_in_d, q * D + c * 128, 128),
                      writes=[("w", i)])
            wk = ("w", i)
            for ti, (t0, n) in enumerate(TILES0):
                hk = tk("hT", t0, n)
                pp = []
                for q in range(3):
                    pst, pk = bank()
                    for k in range(8):
                        mm(pst[:, 0:n], w[:, k, q * 128:(q + 1) * 128], hT[:, k, t0:t0 + n], k == 0, k == 7, [wk] + hk,
                           [pk])
                    pp.append((pst, pk))
                j = 0
                act(gb[:, t0:t0 + n], pp[0][0][:, 0:n], AF.Copy, [pp[0][1]], tk("sc_gb", t0, n))
                act(ub[j][:, 0:n], pp[2][0][:, 0:n], AF.Copy, [pp[2][1]], [("sc_u", j)])
                tt(cu[:, t0:t0 + n], pp[1][0][:, 0:n], ub[j][:, 0:n], ALU.mult, [pp[1][1], ("sc_u", j)],
                   tk("sc_cu", t0, n))
            ts(cu[:, NPR:NPR + 64], cu[:, NPR:NPR + 64], hmask[:, 0:1], None, ALU.mult, None,
               tk("sc_cu", NPR, 64) + ["hmask"], tk("sc_cu", NPR, 64))
            ts(cu[:, NT - 64:NT], cu[:, NT - 64:NT], hmask[:, 1:2], None, ALU.mult, None,
               tk("sc_cu", NT - 64, 64) + ["hmask"], tk("sc_cu", NT - 64, 64))
            rk = tk("sc_cu", 0, NPR)
            zk = tk("sc_z", 0, NPR)
            act(zb[:, 0:NPR], cu[:, 0:NPR], AF.Identity, rk + ["scw", "scb"], zk, scale=scw[:, 1, c:c + 1],
                bias=scb[:, c:c + 1])
            z3 = zb[:, 0:NPR].rearrange("p (s t) -> p s t", t=256)
            c3 = cu[:, 0:NPR].rearrange("p (s t) -> p s t", t=256)
            stt(z3[:, :, 1:256], c3[:, :, 0:255], scw[:, 0, c:c + 1], z3[:, :, 1:256], ALU.mult, ALU.add, rk + zk, zk)
            stt(z3[:, :, 0:255], c3[:, :, 1:256], scw[:, 2, c:c + 1], z3[:, :, 0:255], ALU.mult, ALU.add, rk + zk, zk)
            tt(pT[:, c, 0:NPR], zb[:, 0:NPR], gb[:, 0:NPR], ALU.mult, zk + tk("sc_gb", 0, NPR), tk("pT", 0, NPR))
            rk = tk("sc_cu", NPR, NSX)
            zk = tk("sc_z", OWN0, 1024)
            act(zb[:, OWN0:OWN0 + 1024], cu[:, OWN0:OWN0 + 1024], AF.Identity, rk + ["scw", "scb"], zk,
                scale=scw[:, 1, c:c + 1], bias=scb[:, c:c + 1])
            stt(zb[:, OWN0:OWN0 + 1024], cu[:, OWN0 - 64:OWN0 + 960], scw[:, 0, c:c + 1], zb[:, OWN0:OWN0 + 1024],
                ALU.mult, ALU.add, rk + zk, zk)
            stt(zb[:, OWN0:OWN0 + 1024], cu[:, OWN0 + 64:OWN0 + 1088], scw[:, 2, c:c + 1], zb[:, OWN0:OWN0 + 1024],
                ALU.mult, ALU.add, rk + zk, zk)
            tt(pT[:, c, OWN0:OWN0 + 1024], zb[:, OWN0:OWN0 + 1024], gb[:, OWN0:OWN0 + 1024], ALU.mult,
               zk + tk("sc_gb", OWN0, 1024), tk("pT", OWN0, 1024))
        out_proj(sw_out_d, pT, "pT", TILES1, l, 2)

    def hgrn_summary():
        b.barrier()
        pools["big"] = [0, 1, 2, 3]
        pools["small"] = [4, 5, 6, 7]
        cv = Carve()
        hb = cv.take([8, 1024], BF16)
        Wv = cv.take([8, 1024], BF16)
        cvn0 = cv.off
        sq = cv.take([8, 256], BF16)
        rs = cv.take([1, 256], F32)[:, 0, :]
        tmpn = [cv.take([1, 256], F32)[:, 0, :] for _ in range(2)]
        NB = 3
        ez = [cv.take([1, 512], F32)[:, 0, :] for _ in range(NB)]
        lf = [cv.take([1, 512], F32)[:, 0, :] for _ in range(NB)]
        eb = [cv.take([1, 512], F32)[:, 0, :] for _ in range(NB)]
        ksb = [cv.take([1, 512], BF16)[:, 0, :] for _ in range(NB)]
        vt = [cv.take([1, 512], BF16)[:, 0, :] for _ in range(NB)]
        dd = [cv.take([1, 4], F32)[:, 0, :] for _ in range(NB)]
        Wf = [wslot[i].rearrange("p k n -> p (k n)")[:, 0:4096].rearrange("p (k n) -> p k n", k=4) for i in range(2)]
        for h in range(8):
            wsrc = hw_in_d[h].rearrange("(k p) n -> p k n", p=128)
            b.dma("pool", "wv", Wv[:, :, h * 128:(h + 1) * 128], wsrc[:, :, 256:384], writes=["Wv"])
        for dr, xd, moff in ((0, xT_pre_d, 9), (1, xT_suf_d, 33)):
            for h in range(8):
                wsrc = hw_in_d[h].rearrange("(k p) n -> p k n", p=128)
                for i in range(2):
                    b.dma("pool", "w%d" % i, Wf[i][:, :, h * 128:(h + 1) * 128],
                          wsrc[:, i * 4:(i + 1) * 4, dr * 128:(dr + 1) * 128], writes=[("w", i)])
            blocks = [0, 1, 2] if dr == 0 else [2, 1, 0]
            for blk in blocks:
                for k in range(8):
                    b.dma("sp", "xst", xstage[:, k, :], xd[k * 128:(k + 1) * 128, blk * 1024:(blk + 1) * 1024],
                          writes=tk("xst", 0, 1024))
                for t0 in range(0, 1024, 256):
                    n = 256
                    rk = tk("xst", t0, n)
                    act(sq[:, :, 0:n], xstage[:, :, t0:t0 + n], AF.Square, rk, ["nm_sq"])
                    pst, pk = bank()
                    for k in range(8):
                        mm(pst[:, 0:n], ones_bf, sq[:, k, 0:n], k == 0, k == 7, ["nm_sq", "ones_bf"], [pk])
                    act(rs[:, 0:n], pst[:, 0:n], AF.Ln, [pk, "epsc"], ["nm_rs"], scale=1.0 / D, bias=epsc[:, 0:1])
                    act(rs[:, 0:n], rs[:, 0:n], AF.Exp, ["nm_rs"], ["nm_rs"], scale=-0.5)
                    for k in range(8):
                        tb = tmpn[k % 2]
                        stt(tb[:, 0:n], xstage[:, k, t0:t0 + n], Amod[:, 0, k, 1:2], rs[:, 0:n], ALU.mult, ALU.mult,
                            rk + ["nm_rs", "Amod"], [("nm_t", k % 2)])
                        act(hb[:, k, t0:t0 + n], tb[:, 0:n], AF.Identity, [("nm_t", k % 2), "modv"],
                            tk("hb", t0, n), bias=modv[:, 0, k, 1:2], scale=1.0)
                tiles_ = list(range(8)) if dr == 0 else list(range(7, -1, -1))
                units = [(tix, hh) for tix in tiles_ for hh in range(2)]
                st = {}

                def S1(u):
                    tix, hh = units[u]
                    j = u % NB
                    t0 = tix * 128
                    c0 = hh * 512
                    pf, pfk = bank()
                    for k in range(8):
                        mm(pf, hb[:, k, t0:t0 + 128], Wf[k // 4][:, k % 4, c0:c0 + 512], k == 0, k == 7,
                           [("w", k // 4)] + tk("hb", t0, 128), [pfk])
                    pv, pvk = bank()
                    for k in range(8):
                        mm(pv, hb[:, k, t0:t0 + 128], Wv[:, k, c0:c0 + 512], k == 0, k == 7,
                           ["Wv"] + tk("hb", t0, 128), [pvk])
                    act(ez[j], pf, AF.Exp, [pfk], [("ez", j)])
                    cp(vt[j], pv, [pvk], [("vt", j)])
                    act(ez[j], ez[j], AF.Ln, [("ez", j), "ones_f"], [("ez", j)], bias=ones_f[:, 0:1], scale=1.0)
                    act(ez[j], ez[j], AF.Exp, [("ez", j)], [("ez", j)], scale=-1.0)
                    tcol = blk * 8 + tix
                    stt(ez[j].rearrange("p (h d) -> p h d", h=4), ez[j].rearrange("p (h d) -> p h d", h=4),
                        maskt[:, moff + tcol:moff + tcol + 1], oml[:, hh * 4:(hh + 1) * 4, dr, :], ALU.mult, ALU.mult,
                        [("ez", j), "maskt", "oml"], [("ez", j)])
                    act(lf[j], ez[j], AF.Ln, [("ez", j), "ones_f"], [("lf", j)], scale=-1.0, bias=ones_f[:, 0:1])

                def S2(u):
                    tix, hh = units[u]
                    j = u % NB
                    pB, pBk = qslot4()
                    mm(pB, tri[:, 4 + dr, :], lf[j], True, True, [("tri", 4 + dr), ("lf", j)], [pBk])
                    pD, pDk = qslot4()
                    for h4 in range(4):
                        mm(pD[:, h4 * 128:h4 * 128 + 2], lf[j][:, h4 * 128:(h4 + 1) * 128], ones_f[:, 0:2], True, True,
                           [("lf", j), "ones_f"], [pDk])
                    act(eb[j], pB, AF.Exp, [pBk], [("eb", j)])
                    act(dd[j], pD.rearrange("p (h d) -> p h d", h=4)[:, :, 0], AF.Exp, [pDk], [("dd", j)])
                    tt(ksb[j], ez[j], eb[j], ALU.mult, [("ez", j), ("eb", j)], [("ksb", j)])

                def S3(u):
                    tix, hh = units[u]
                    j = u % NB
                    pKV, pKVk = qslot4()
                    for h4 in range(4):
                        mm(pKV[:, h4 * 128:(h4 + 1) * 128], ksb[j][:, h4 * 128:(h4 + 1) * 128],
                           vt[j][:, h4 * 128:(h4 + 1) * 128], True, True, [("ksb", j), ("vt", j)], [pKVk])
                    for h4 in range(4):
                        h = hh * 4 + h4
                        stt(Sinit[:, dr, h, :], Sinit[:, dr, h, :], dd[j][:, h4:h4 + 1],
                            pKV[:, h4 * 128:(h4 + 1) * 128], ALU.mult, ALU.add,
                            [pKVk, ("dd", j), ("Sinit", dr, h)], [("Sinit", dr, h)])

                nu = len(units)
                for i in range(nu + 2):
                    if i < nu:
                        S1(i)
                    if 0 <= i - 1 < nu:
                        S2(i - 1)
                    if 0 <= i - 2 < nu:
                        S3(i - 2)

    def hgrn_main(l):
        regions = [
            dict(t0=0, ntile=8, seqs=[(0, 2), (2, 2), (4, 2), (6, 2)], masked=False, init=False, out_state=True),
            dict(t0=NPR, ntile=9, seqs=[(0, 9)], masked=True, init=True, out_state=False),
        ]
        for R in regions:
            b.barrier()
            cv = Carve()
            nt_ = R["ntile"]
            T0 = R["t0"]
            ntok = nt_ * 128
            hT = cv.take([8, ntok], BF16)
            rtiles = [(T0 + a_, n_) for (a_, n_) in
                      ([(0, 512), (512, 512)] if T0 == 0 else [(0, 384), (384, 384), (768, 384)])]
            cvn = Carve()
            cvn.off = cv.off
            sq = cvn.take([8, 512], BF16)
            rs = cvn.take([1, 512], F32)[:, 0, :]
            tmpb = [cvn.take([1, 512], F32)[:, 0, :] for _ in range(2)]
            for (t0, n) in rtiles:
                v = 0 if t0 < NPR else 1
                rk = tk("xT", t0, n)
                act(sq[:, :, 0:n], xT[:, :, t0:t0 + n], AF.Square, rk, ["nm_sq"])
                pst, pk = bank()
                for k in range(8):
                    mm(pst[:, 0:n], ones_bf, sq[:, k, 0:n], k == 0, k == 7, ["nm_sq", "ones_bf"], [pk])
                act(rs[:, 0:n], pst[:, 0:n], AF.Ln, [pk, "epsc"], ["nm_rs"], scale=1.0 / D, bias=epsc[:, 0:1])
                act(rs[:, 0:n], rs[:, 0:n], AF.Exp, ["nm_rs"], ["nm_rs"], scale=-0.5)
                for k in range(8):
                    tb = tmpb[k % 2]
                    stt(tb[:, 0:n], xT[:, k, t0:t0 + n], Amod[:, 2 * l, k, v:v + 1], rs[:, 0:n], ALU.mult, ALU.mult,
                        rk + ["nm_rs", "Amod"], [("nm_t", k % 2)])
                    act(hT[:, k, t0 - T0:t0 - T0 + n], tb[:, 0:n], AF.Identity, [("nm_t", k % 2), "modv"],
                        tk("hTr", t0 - T0, n), bias=modv[:, l, k, v:v + 1], scale=1.0)
            b.barrier()
            pools["big"] = [0, 1]
            pools["small"] = [2, 3, 4, 5]
            cvb = Carve(base_f32=actB.bitcast(F32), base_bf=actB, limit=8 * NT * 2)
            P = []
            for i2 in range(2):
                P.append(dict(qd=cvb.take([2, NSX], BF16), scm=cvb.take([2, 9, 128], BF16),
                              vtm=cvb.take([9, 128], BF16), ksm=cvb.take([2, 9, 128], BF16),
                              gs=cv.take([1, NSX], BF16)[:, 0, :], Dc=cv.take([2, 36], F32),
                              oTh=cv.take([1, NSX], BF16)[:, 0, :], wo=cv.take([1, D], BF16)[:, 0, :]))
            qs = cv.take([1, NSX], BF16)[:, 0, :]
            SprevB = cv.take([9 * CPT, 128], BF16)
            RING = 2 * CPT
            NSF = 2 * RING if len(R["seqs"]) > 1 else RING
            SprevF = cv.take([NSF, 128], BF16)
            Sst = cv.take([6, 128], F32)
            NB = 2
            ki = [cv.take([2, 128], BF16) for _ in range(2)]
            kt = [cv.take([1, 256], F32)[:, 0, :] for _ in range(NB)]
            lf = [cv.take([1, 256], F32)[:, 0, :] for _ in range(NB)]
            eg = [cv.take([2, 128], F32) for _ in range(2)]
            en = [cv.take([2, 128], F32) for _ in range(2)]
            ebb = [cv.take([2, 128], F32) for _ in range(2)]
            osq = [cv.take([1, 128], BF16)[:, 0, :] for _ in range(2)]
            rso = [cv.take([1, 128], F32)[:, 0, :] for _ in range(2)]
            o1 = cv.take([1, 128], F32)[:, 0, :]
            ocnt = [0]
            wcache = {}

            def get_w(h):
                if h not in wcache and h < 8:
                    wsrc = hw_in_d[h].rearrange("(k p) n -> p k n", p=128)
                    wcache[h] = wload(wsrc, 640)
                return wcache.get(h)

            def make_head(h):
                pb = P[h % 2]
                hp = h % 2
                qd, scm, vtm, ksm, gs, Dc, oTh, wo = (pb[k_] for k_ in ("qd", "scm", "vtm", "ksm", "gs", "Dc", "oTh", "wo"))
                w, wk = get_w(h)

                def A():
                    b.dma("pool", "wo%d" % hp, wo, hw_out_d[h * 128:(h + 1) * 128, :], writes=[("wo", hp)])
                    get_w(h + 1)
                    pools["big"] = [0, 1, 2, 3, 4, 5]
                    for (t0, n) in rtiles:
                        lt = t0 - T0
                        hk = tk("hTr", lt, n)
                        pq, pqk = bank()
                        for k in range(8):
                            mm(pq[:, 0:n], w[:, k, 384:512], hT[:, k, lt:lt + n], k == 0, k == 7, [wk] + hk, [pqk])
                        act(qs[:, lt:lt + n], pq[:, 0:n], AF.Silu, [pqk], tk("qs", lt, n))
                        pg, pgk = bank()
                        for k in range(8):
                            mm(pg[:, 0:n], w[:, k, 512:640], hT[:, k, lt:lt + n], k == 0, k == 7, [wk] + hk, [pgk])
                        act(gs[:, lt:lt + n], pg[:, 0:n], AF.Silu, [pgk], tk(("gs", hp), lt, n))
                    pools["big"] = [0, 1]

                def T1(tix):
                    lt = tix * 128
                    j = tix % NB
                    pz, pzk = bank()
                    for k in range(8):
                        mm(pz[:, 0:384], hT[:, k, lt:lt + 128], w[:, k, 0:384], k == 0, k == 7,
                           [wk] + tk("hTr", lt, 128), [pzk])
                    act(kt[j], pz[:, 0:256], AF.Exp, [pzk], [("kt", j)])
                    act(vtm[:, tix, :], pz[:, 256:384], AF.Copy, [pzk], [("vtm", hp, tix)])
                    act(kt[j], kt[j], AF.Ln, [("kt", j), "ones_f"], [("kt", j)], bias=ones_f[:, 0:1], scale=1.0)
                    act(kt[j], kt[j], AF.Exp, [("kt", j)], [("kt", j)], scale=-1.0)
                    omlh = oml[:, h, :, :].rearrange("p a d -> p (a d)")
                    if R["masked"]:
                        stt(kt[j], kt[j], maskt[:, tix:tix + 1], omlh, ALU.mult, ALU.mult,
                            [("kt", j), "maskt", "oml"], [("kt", j)])
                    else:
                        tt(kt[j], kt[j], omlh, ALU.mult, [("kt", j), "oml"], [("kt", j)])
                    act(lf[j], kt[j], AF.Ln, [("kt", j), "ones_f"], [("lf", j)], scale=-1.0, bias=ones_f[:, 0:1])

                def T2(tix):
                    lt = tix * 128
                    j = tix % NB
                    jj = tix % 2
                    pG, pGk = qslot4()
                    pB, pBk = qslot4()
                    for dr in range(2):
                        lfd = lf[j][:, dr * 128:(dr + 1) * 128]
                        mm(pG[:, dr * 128:(dr + 1) * 128], lfd, tri[:, dr, :], True, True, [("lf", j), ("tri", dr)], [pGk])
                    for dr in range(2):
                        lfd = lf[j][:, dr * 128:(dr + 1) * 128]
                        mm(pB[:, dr * 128:(dr + 1) * 128], tri[:, 2 + dr, :], lfd, True, True,
                           [("lf", j), ("tri", 2 + dr)], [pBk])
                    egf = eg[jj].rearrange("p a d -> p (a d)")
                    enf = en[jj].rearrange("p a d -> p (a d)")
                    ebf = ebb[jj].rearrange("p a d -> p (a d)")
                    act(egf, pG[:, 0:256], AF.Exp, [pGk], [("eg", jj)])
                    act(enf, pG[:, 0:256], AF.Exp, [pGk], [("en", jj)], scale=-1.0)
                    act(ebf, pB[:, 0:256], AF.Exp, [pBk], [("ebb", jj)])
                    pK, pKk = qslot4()
                    for dr in range(2):
                        ktd = kt[j][:, dr * 128:(dr + 1) * 128]
                        b.op("pe", lambda e, dr=dr, ktd=ktd: e.transpose(pK[:, dr * 128:(dr + 1) * 128], ktd, ident),
                             [("kt", j), "ident"], [pKk])
                    tt(qd[:, :, lt:lt + 128], qs[:, lt:lt + 128].unsqueeze(1).to_broadcast([128, 2, 128]), eg[jj],
                       ALU.mult, tk("qs", lt, 128) + [("eg", jj)], [("qd", hp, 0, tix), ("qd", hp, 1, tix)])
                    tt(ki[jj].rearrange("p a d -> p (a d)"), pK[:, 0:256], enf, ALU.mult, [pKk, ("en", jj)],
                       [("ki", jj)])
                    tt(ksm[:, :, tix, :], kt[j].rearrange("p (a d) -> p a d", a=2), ebb[jj], ALU.mult,
                       [("kt", j), ("ebb", jj)], [("ksm", hp, 0, tix), ("ksm", hp, 1, tix)], e="pool")
                    for dr in range(2):
                        col = CH - 1 if dr == 0 else 0
                        cp(Dc[:, dr, tix * CPT:(tix + 1) * CPT],
                           eg[jj][:, dr, :].rearrange("p (c t) -> p c t", t=CH)[:, :, col],
                           [("eg", jj)], [("Dc", hp, dr, tix)], e="pool")

                def T3(tix):
                    lt = tix * 128
                    jj = tix % 2
                    pS, pSk = qslot4()
                    for dr in range(2):
                        mm(pS[:, dr * 128:(dr + 1) * 128], ki[jj][:, dr, :], qd[:, dr, lt:lt + 128], True, True,
                           [("ki", jj), ("qd", hp, dr, tix)], [pSk])
                    tt(scm[:, :, tix, :], pS[:, 0:256].rearrange("p (a c) -> p a c", a=2), tri[:, 0:2, :], ALU.mult,
                       [pSk, ("tri", 0), ("tri", 1)], [("scm", hp, 0, tix), ("scm", hp, 1, tix)])

                def tstep(i):
                    def f():
                        if i < nt_:
                            T1(i)
                        if 0 <= i - 1 < nt_:
                            T2(i - 1)
                        if 0 <= i - 2 < nt_:
                            T3(i - 2)
                    return f
                Tl = [tstep(i) for i in range(nt_ + 2)]

                def out_stages(tix, fslot):
                    lt = tix * 128
                    jo = ocnt[0] % 2
                    ocnt[0] += 1
                    po, pok = ps[6 + jo][:, 0:128], ("ps", 6 + jo)

                    def O1():
                        seq = [(vtm[:, tix, :], scm[:, dr, tix, :], [("vtm", hp, tix), ("scm", hp, dr, tix)], None)
                               for dr in range(2)]
                        for c4 in range(CPT):
                            gc = tix * CPT + c4
                            fs = fslot + c4
                            seq.append((SprevF[:, fs, :], qd[:, 0, lt + c4 * CH:lt + c4 * CH + CH],
                                        [("SprevF", fs), ("qd", hp, 0, tix)], c4))
                            seq.append((SprevB[:, gc, :], qd[:, 1, lt + c4 * CH:lt + c4 * CH + CH],
                                        [("SprevB", gc), ("qd", hp, 1, tix)], c4))
                        for qi, (lh, rh, rd, c4) in enumerate(seq):
                            outp = po if c4 is None else po[:, c4 * CH:c4 * CH + CH]
                            mm(outp, lh, rh, qi == 0, qi == len(seq) - 1, rd, [pok])
                        act(osq[jo], po, AF.Square, [pok], [("osq", jo)])

                    def O2():
                        pn, pnk = qslot()
                        mm(pn, ones_bf, osq[jo], True, True, [("osq", jo), "ones_bf"], [pnk])
                        act(rso[jo], pn, AF.Ln, [pnk, "epsc"], [("rso", jo)], scale=1.0 / 128, bias=epsc[:, 0:1])
                        act(rso[jo], rso[jo], AF.Exp, [("rso", jo)], [("rso", jo)], scale=-0.5)

                    def O3():
                        tt(o1, po, rso[jo], ALU.mult, [pok, ("rso", jo)], ["o1"])
                        stt(oTh[:, lt:lt + 128], o1, gnw[:, 0:1], gs[:, lt:lt + 128], ALU.mult, ALU.mult,
                            ["o1", "gnw"] + tk(("gs", hp), lt, 128), tk(("oTh", hp), lt, 128))
                    return [O1, O2, O3]

                Sl = []
                seqs = R["seqs"]
                for p0 in range(0, len(seqs), 2):
                    grp = list(enumerate(seqs))[p0:p0 + 2]
                    for dr in (1, 0):
                        ctx = {"state": {}, "pending": [], "step": 0}

                        def init(grp=grp, dr=dr, ctx=ctx):
                            for sl, (si, (s0t, snt)) in enumerate(grp):
                                S = Sst[:, sl * 3, :]
                                Sk = ("Sst", sl, 0)
                                if R["init"]:
                                    cp(S, Sinit[:, dr, h, :], [("Sinit", dr, h)], [Sk], e="pool")
                                else:
                                    b.op("pool", lambda g, S=S: g.memset(S, 0.0), [], [Sk])
                                ctx["state"][sl] = 0
                        Sl.append(init)
                        nch = grp[0][1][1] * CPT
                        order = list(range(nch)) if dr == 0 else list(range(nch - 1, -1, -1))
                        for cn in order:
                            for sl, (si, (s0t, snt)) in enumerate(grp):
                                def stepf(cn=cn, sl=sl, s0t=s0t, dr=dr, ctx=ctx):
                                    ctx["step"] += 1
                                    step = ctx["step"]
                                    pp = ctx["state"][sl]
                                    S = Sst[:, sl * 3 + pp, :]
                                    Sk = ("Sst", sl, pp)
                                    S2 = Sst[:, sl * 3 + (pp + 1) % 3, :]
                                    S2k = ("Sst", sl, (pp + 1) % 3)
                                    gc = s0t * CPT + cn
                                    tix = gc // CPT
                                    r0 = (gc % CPT) * CH
                                    if dr == 1:
                                        dstS, dstk = SprevB[:, gc, :], ("SprevB", gc)
                                    else:
                                        fs = (sl * RING + cn % RING) % NSF
                                        dstS, dstk = SprevF[:, fs, :], ("SprevF", fs)
                                    if step % 3 == 0:
                                        cp(dstS, S, [Sk], [dstk], e="pool")
                                    else:
                                        act(dstS, S, AF.Copy, [Sk], [dstk])
                                    pkv, pkvk = qslot()
                                    mm(pkv, ksm[r0:r0 + CH, dr, tix, :], vtm[r0:r0 + CH, tix, :], True, True,
                                       [("ksm", hp, dr, tix), ("vtm", hp, tix)], [pkvk], tile_position=(r0, 0))
                                    stt(S2, S, Dc[:, dr, gc:gc + 1], pkv, ALU.mult, ALU.add,
                                        [pkvk, ("Dc", hp, dr, tix), Sk], [S2k])
                                    ctx["state"][sl] = (pp + 1) % 3
                                    if dr == 0 and gc % CPT == CPT - 1:
                                        st3 = out_stages(tix, (sl * RING + (cn - (CPT - 1)) % RING) % NSF)
                                        ctx["pending"].append((step + 1, st3[0]))
                                        ctx["pending"].append((step + 2, st3[1]))
                                        ctx["pending"].append((step + 3, st3[2]))
                                    due = [f for (t_, f) in ctx["pending"] if t_ <= step]
                                    ctx["pending"] = [(t_, f) for (t_, f) in ctx["pending"] if t_ > step]
                                    for f in due:
                                        f()
                                Sl.append(stepf)

                        def fin(grp=grp, dr=dr, ctx=ctx):
                            for (t_, f) in ctx["pending"]:
                                f()
                            ctx["pending"] = []
                            if R["out_state"]:
                                for sl, (si, (s0t, snt)) in enumerate(grp):
                                    pp = ctx["state"][sl]
                                    b.dma("sp", "nso", ns_d[si, dr, h], Sst[:, sl * 3 + pp, :],
                                          reads=[("Sst", sl, pp)], writes=[("nsd", si, dr, h)])
                        Sl.append(fin)

                def wout():
                    for m in range(8):
                        for (t0, n) in rtiles:
                            lt = t0 - T0
                            v = 0 if t0 < NPR else 1
                            pst, pk = bank()
                            mm(pst[:, 0:n], wo[:, m * 128:(m + 1) * 128], oTh[:, lt:lt + n], True, True,
                               [("wo", hp)] + tk(("oTh", hp), lt, n), [pk])
                            stt(xT[:, m, t0:t0 + n], pst[:, 0:n], modv[:, l, 16 + m, v:v + 1], xT[:, m, t0:t0 + n],
                                ALU.mult, ALU.add, [pk, "modv"] + tk("xT", t0, n), tk("xT", t0, n))
                Sl.append(wout)
                return A, Tl, Sl

            prevS = []
            for h in range(8):
                A, Tl, Sl = make_head(h)
                A()
                nT = len(Tl)
                nS = len(prevS)
                si_ = 0
                for ti_, tf in enumerate(Tl):
                    tf()
                    upto_ = (nS * (ti_ + 1)) // nT
                    while si_ < upto_:
                        prevS[si_]()
                        si_ += 1
                prevS = Sl
            for f in prevS:
                f()
            pools["big"] = [0, 1, 2, 3, 4, 5, 6, 7]
            pools["small"] = [4, 5, 6, 7]

    def dump(i):
        if debug:
            for k in range(8):
                b.dma("sp", "dbg", dbg_d[i, k * 128:(k + 1) * 128, :], xT[:, k, :], reads=tk("xT", 0, NT),
                      writes=[("dbg", i, k)])

    if upto >= 2:
        hgrn_summary()
    if upto >= 3:
        hgrn_main(0)
    dump(0)
    if upto >= 4:
        conv_ffn(0, TILES0)
    dump(1)
    if upto >= 5:
        sconv(1)
    dump(2)
    if upto >= 6:
        conv_ffn(1, TILES1)
    dump(3)
    b.barrier()
    cvf = Carve()
    fo = arena[:, 0:8 * 1024].rearrange("p (k n) -> p k n", k=8)
    cvf.off = 8 * 1024 * 4
    for half, (tiles, dd, base) in enumerate([(TILES1[0:2], yT_p_d, 0), (TILES1[2:4], yT_s_d, OWN0)]):
        cvh = Carve()
        cvh.off = cvf.off

        class V:
            pass
        sq = cvh.take([8, 512], BF16)
        rs = cvh.take([1, 512], F32)[:, 0, :]
        for (t0, n) in tiles:
            rk = tk("xT", t0, n)
            act(sq[:, :, 0:n], xT[:, :, t0:t0 + n], AF.Square, rk, ["nm_sq"])
            pst, pk = bank()
            for k in range(8):
                mm(pst[:, 0:n], ones_bf, sq[:, k, 0:n], k == 0, k == 7, ["nm_sq", "ones_bf"], [pk])
            act(rs[:, 0:n], pst[:, 0:n], AF.Ln, [pk, "epsc"], ["nm_rs"], scale=1.0 / D, bias=epsc[:, 0:1])
            act(rs[:, 0:n], rs[:, 0:n], AF.Exp, ["nm_rs"], ["nm_rs"], scale=-0.5)
            for k in range(8):
                stt(fo[:, k, t0 - base:t0 - base + n], xT[:, k, t0:t0 + n], nfin[:, k:k + 1], rs[:, 0:n], ALU.mult,
                    ALU.mult, rk + ["nm_rs", "nfin"], [("fo", k)])
        for k in range(8):
            b.dma("sp", "yout", dd[k * 128:(k + 1) * 128, :], fo[:, k, :], reads=[("fo", k)], writes=[("yo", half, k)])
    b.wait_all("sp")
    return nc, es


_CACHE = {}


def _prep_inputs(inp):
    f = np.float32
    x_prompt = np.asarray(inp["x_prompt"], f)
    x_sample = np.asarray(inp["x_sample"], f)
    state = np.asarray(inp["state_hgrn"], f)
    c = np.asarray(inp["c"], f)
    c_ctx = np.asarray(inp["c_ctx"], f)

    def fm(vec, nchunk):
        v = vec.reshape(vec.shape[:-1] + (nchunk, 128))
        return np.ascontiguousarray(np.moveaxis(v, -1, 0))

    shared = {}
    shared["ada_w"] = np.ascontiguousarray(inp["ada_w"], f)
    shared["ada_b"] = fm(np.asarray(inp["ada_b"], f), 48)
    shared["nmix"] = fm(np.asarray(inp["norm_mix_w"], f), 8)
    shared["nffn"] = fm(np.asarray(inp["norm_ffn_w"], f), 8)
    shared["nfin"] = fm(np.asarray(inp["final_norm_w"], f), 8)
    w_in = np.asarray(inp["hgrn_w_in"], f)[0]
    hw = np.empty((8, D, 640), f)
    for h in range(8):
        for q, base in enumerate([1024, 2048, 3072, 0, 4096]):
            hw[h, :, q * 128:(q + 1) * 128] = w_in[:, base + h * 128:base + (h + 1) * 128]
    shared["hw_in"] = hw
    shared["lbraw"] = np.ascontiguousarray(inp["hgrn_lower_bounds"], f)
    shared["gnw"] = np.ascontiguousarray(np.asarray(inp["hgrn_gnorm_w"], f)[0].reshape(128, 1))
    shared["hw_out"] = np.ascontiguousarray(np.asarray(inp["hgrn_w_out"], f)[0])
    shared["sw_in"] = np.ascontiguousarray(np.asarray(inp["sconv_w_in"], f)[0])
    shared["scw"] = np.ascontiguousarray(np.moveaxis(np.asarray(inp["sconv_conv_w"], f)[0].reshape(3, 8, 128), -1, 0))
    shared["scb"] = fm(np.asarray(inp["sconv_conv_b"], f)[0], 8)
    shared["sw_out"] = np.ascontiguousarray(np.asarray(inp["sconv_w_out"], f)[0])
    shared["fw_up"] = np.ascontiguousarray(inp["ffn_w_up"], f)
    shared["fcw"] = np.ascontiguousarray(np.moveaxis(np.asarray(inp["ffn_conv_w"], f).reshape(2, 3, 22, 128), -1, 0))
    shared["fcb"] = fm(np.asarray(inp["ffn_conv_b"], f), 22)
    shared["fw_dn"] = np.ascontiguousarray(inp["ffn_w_down"], f)

    maps = []
    for core in range(8):
        bi, j = core // 4, core % 4
        m = dict(shared)
        xp = x_prompt[4 * core:4 * core + 4].reshape(NPR, D)
        m["xT_p"] = np.ascontiguousarray(xp.T)
        lo, hi = 1024 * j - 64, 1024 * (j + 1) + 64
        xs = np.zeros((NSX, D), f)
        mask_s = np.zeros(NSX, f)
        a, e = max(lo, 0), min(hi, 4096)
        xs[a - lo:e - lo] = x_sample[bi, a:e]
        mask_s[a - lo:e - lo] = 1.0
        m["xT_s"] = np.ascontiguousarray(xs.T)
        pre = np.zeros((NREG, D), f)
        mask_pre = np.zeros(NREG, f)
        npre = max(lo, 0)
        if npre > 0:
            pre[NREG - npre:] = x_sample[bi, 0:npre]
            mask_pre[NREG - npre:] = 1.0
        m["xT_pre"] = np.ascontiguousarray(pre.T)
        suf = np.zeros((NREG, D), f)
        mask_suf = np.zeros(NREG, f)
        nsuf = max(4096 - hi, 0)
        if nsuf > 0:
            suf[0:nsuf] = x_sample[bi, hi:4096]
            mask_suf[0:nsuf] = 1.0
        m["xT_suf"] = np.ascontiguousarray(suf.T)
        mk_ = np.concatenate([mask_s.reshape(9, 128), mask_pre.reshape(24, 128), mask_suf.reshape(24, 128)], 0)
        m["mask_tm"] = np.ascontiguousarray(mk_.T)
        hm = np.zeros((128, 2), f)
        hm[:, 0] = 1.0 if lo >= 0 else 0.0
        hm[:, 1] = 1.0 if hi <= 4096 else 0.0
        m["hmask"] = hm
        m["s0"] = np.ascontiguousarray(np.transpose(state[bi, 0], (2, 0, 1, 3)))
        cc = np.stack([c_ctx, c[bi]], -1).reshape(8, 128, 2)
        m["cT"] = np.ascontiguousarray(np.transpose(cc, (1, 0, 2)))
        maps.append(m)
    return maps


def kernel(**inputs):
    debug = bool(inputs.pop("_debug", False))
    upto = int(inputs.pop("_upto", 99))
    key = ("prog", debug, upto)
    if key not in _CACHE:
        _CACHE[key] = build_program(debug, upto)
    nc, _es = _CACHE[key]
    maps = _prep_inputs(inputs)
    res = run_bass_kernel_spmd(nc, maps, core_ids=list(range(8)))
    r = res.results
    y_prompt = np.empty((32, 256, D), np.float32)
    y_sample = np.empty((2, 4096, D), np.float32)
    new_state = np.empty((32, 1, 2, 8, 128, 128), np.float32)
    for core in range(8):
        bi, j = core // 4, core % 4
        y_prompt[4 * core:4 * core + 4] = r[core]["yT_p"].T.reshape(4, 256, D)
        y_sample[bi, 1024 * j:1024 * (j + 1)] = r[core]["yT_s"].T
        new_state[4 * core:4 * core + 4, 0] = r[core]["ns"]
    if debug:
        return (y_prompt, y_sample, new_state), [r[c]["dbg"] for c in range(8)]
    return (y_prompt, y_sample, new_state)
```
